# Optimizing a Trainium2 kernel written in Bass

```python
import jax, jax.numpy as jnp
from jax import lax
import numpy as np

D_MODEL = 4096
BATCH = 1
SEQ = 8192
DEPTH = 1

GRID_W = 64
CTX_LEN = 256
CHUNK = 128
HEAD_DIM = 128
MIX_WIDTH = D_MODEL
A_WIDTH = MIX_WIDTH // 2
A_GROUPS = A_WIDTH // HEAD_DIM
NA_WIDTH = MIX_WIDTH - A_WIDTH
NA_HEADS = NA_WIDTH // HEAD_DIM
NA_KH = 8
NA_KW = 16
IN_COLS = 2 * A_WIDTH + 3 * NA_WIDTH
KV_START = 2 * A_WIDTH + NA_WIDTH
D_FF = 256 * ((8 * D_MODEL // 3 + 255) // 256)
CONV_W = 3
EPS = 1e-6

kernel_name = "hybrid_gmlp_natten_dit_block"


def rmsnorm(x, g):
    x32 = x.astype(jnp.float32)
    y = x32 * lax.rsqrt(jnp.mean(x32 * x32, axis=-1, keepdims=True) + EPS)
    return (y * g.astype(jnp.float32)).astype(x.dtype)


def layernorm(x, g, b):
    x32 = x.astype(jnp.float32)
    mu = jnp.mean(x32, axis=-1, keepdims=True)
    var = jnp.mean(jnp.square(x32 - mu), axis=-1, keepdims=True)
    y = (x32 - mu) * lax.rsqrt(var + EPS)
    return (y * g.astype(jnp.float32) + b.astype(jnp.float32)).astype(x.dtype)


def adaln_params(cvec, w, b):
    m = jax.nn.silu(cvec) @ w + b
    return jnp.split(m[:, None, :], 6, axis=-1)


def modulate(h, shift, scale):
    return h * (1 + scale) + shift


def chunk_gmlp(u, v, ln_g, ln_b, w_s, b_s):
    B, N, _ = u.shape
    u = jax.nn.gelu(u)
    v = layernorm(jax.nn.gelu(v), ln_g, ln_b)
    v = v.reshape(B, N // CHUNK, CHUNK, A_GROUPS, HEAD_DIM)
    s = jnp.einsum('gpq,bnqgd->bnpgd', w_s, v) + b_s.T[None, None, :, :, None]
    return u * s.reshape(B, N, A_WIDTH)


def neighbourhood_attention(q, k, v, k_ctx, v_ctx, rpb):
    B, N, H, Dh = q.shape
    rows = N // GRID_W
    kh = min(NA_KH, rows)
    scale = Dh ** -0.5
    r = jnp.arange(rows)
    r0 = jnp.clip(r - kh // 2, 0, rows - kh)
    band = r0[:, None] + jnp.arange(kh)
    qg = q.reshape(B, rows, GRID_W, H, Dh)
    kg = k.reshape(B, rows, GRID_W, H, Dh)[:, band]
    vg = v.reshape(B, rows, GRID_W, H, Dh)[:, band]
    col = jnp.arange(GRID_W)
    c0 = jnp.clip(col - NA_KW // 2, 0, GRID_W - NA_KW)
    col_in = (col[None, :] >= c0[:, None]) & (col[None, :] < c0[:, None] + NA_KW)
    dr_idx = band - r[:, None] + (NA_KH - 1)
    dc_idx = jnp.clip(col[None, :] - col[:, None], -(NA_KW - 1), NA_KW - 1) + (NA_KW - 1)
    bias = rpb[:, dr_idx[:, None, :, None], dc_idx[None, :, None, :]]
    s_loc = jnp.einsum('brqhd,brikhd->bhrqik', qg, kg).astype(jnp.float32) * scale
    s_loc = s_loc + bias[None].astype(jnp.float32)
    s_loc = jnp.where(col_in[:, None, :], s_loc, -jnp.inf)
    s_ctx = jnp.einsum('brqhd,bchd->bhrqc', qg, k_ctx).astype(jnp.float32) * scale
    n_loc = kh * GRID_W
    s = jnp.concatenate([s_loc.reshape(B, H, rows, GRID_W, n_loc), s_ctx], axis=-1)
    p = jax.nn.softmax(s, axis=-1).astype(v.dtype)
    p_loc = p[..., :n_loc].reshape(B, H, rows, GRID_W, kh, GRID_W)
    p_ctx = p[..., n_loc:]
    out = jnp.einsum('bhrqik,brikhd->brqhd', p_loc, vg) + jnp.einsum('bhrqc,bchd->brqhd', p_ctx, v_ctx)
    return out.reshape(B, N, H * Dh)


def context_attention(q, k, v):
    s = jnp.einsum('bqhd,bkhd->bhqk', q, k).astype(jnp.float32) * (q.shape[-1] ** -0.5)
    p = jax.nn.softmax(s, axis=-1).astype(v.dtype)
    B, C = q.shape[0], q.shape[1]
    return jnp.einsum('bhqk,bkhd->bqhd', p, v).reshape(B, C, -1)


def conv_ffn(h, w_up, conv_w, conv_b, w_down):
    N = h.shape[1]
    a, g = jnp.split(h @ w_up, 2, axis=-1)
    pad = CONV_W // 2
    ap = jnp.pad(a, ((0, 0), (pad, pad), (0, 0)))
    a = conv_b + sum(ap[:, t:t + N] * conv_w[t] for t in range(CONV_W))
    return (jax.nn.silu(a) * g) @ w_down


def setup_inputs(seed: int = 0) -> dict:
    key = jax.random.key(seed)
    ks = jax.random.split(key, 22)
    f32 = jnp.float32

    def nrm(k, shape, s):
        return jax.random.normal(k, shape, f32) * s

    return {
        "x": nrm(ks[0], (BATCH, SEQ, D_MODEL), 1.0),
        "c": nrm(ks[1], (BATCH, D_MODEL), 1.0),
        "ctx": nrm(ks[2], (BATCH, CTX_LEN, D_MODEL), 1.0),
        "c_ctx": nrm(ks[3], (D_MODEL,), 1.0),
        "w_ada": nrm(ks[4], (DEPTH, D_MODEL, 6 * D_MODEL), D_MODEL ** -0.5),
        "b_ada": nrm(ks[5], (DEPTH, 6 * D_MODEL), 0.02),
        "g_norm1": 1.0 + nrm(ks[6], (DEPTH, D_MODEL), 0.02),
        "w_in": nrm(ks[7], (DEPTH, D_MODEL, IN_COLS), D_MODEL ** -0.5),
        "a_ln_g": 1.0 + nrm(ks[8], (DEPTH, A_WIDTH), 0.02),
        "a_ln_b": nrm(ks[9], (DEPTH, A_WIDTH), 0.02),
        "a_w_s": nrm(ks[10], (DEPTH, A_GROUPS, CHUNK, CHUNK), CHUNK ** -0.5),
        "a_b_s": 1.0 + nrm(ks[11], (DEPTH, A_GROUPS, CHUNK), 0.02),
        "na_rpb": nrm(ks[12], (DEPTH, NA_HEADS, 2 * NA_KH - 1, 2 * NA_KW - 1), 0.1),
        "w_out": nrm(ks[13], (DEPTH, MIX_WIDTH, D_MODEL), MIX_WIDTH ** -0.5),
        "g_norm2": 1.0 + nrm(ks[14], (DEPTH, D_MODEL), 0.02),
        "w_up": nrm(ks[15], (DEPTH, D_MODEL, 2 * D_FF), D_MODEL ** -0.5),
        "conv_w": nrm(ks[16], (DEPTH, CONV_W, D_FF), CONV_W ** -0.5),
        "conv_b": nrm(ks[17], (DEPTH, D_FF), 0.02),
        "w_down": nrm(ks[18], (DEPTH, D_FF, D_MODEL), D_FF ** -0.5),
        "g_final": 1.0 + nrm(ks[19], (D_MODEL,), 0.02),
    }


def reference(x, c, ctx, c_ctx, w_ada, b_ada, g_norm1, w_in, a_ln_g, a_ln_b, a_w_s, a_b_s,
              na_rpb, w_out, g_norm2, w_up, conv_w, conv_b, w_down, g_final):
    B, N, _ = x.shape
    for l in range(DEPTH):
        sh1, sc1, gt1, sh2, sc2, gt2 = adaln_params(c, w_ada[l], b_ada[l])
        csh1, csc1, cgt1, csh2, csc2, cgt2 = adaln_params(c_ctx[None], w_ada[l], b_ada[l])

        h = modulate(rmsnorm(x, g_norm1[l]), sh1, sc1)
        hc = modulate(rmsnorm(ctx, g_norm1[l]), csh1, csc1)
        u, v, q, k, va = jnp.split(h @ w_in[l], [A_WIDTH, 2 * A_WIDTH, 2 * A_WIDTH + NA_WIDTH,
                                                KV_START + NA_WIDTH], axis=-1)
        kc, vc = jnp.split(hc @ w_in[l][:, KV_START:], 2, axis=-1)
        C = ctx.shape[1]
        kc = kc.reshape(B, C, NA_HEADS, HEAD_DIM)
        vc = vc.reshape(B, C, NA_HEADS, HEAD_DIM)
        y_a = chunk_gmlp(u, v, a_ln_g[l], a_ln_b[l], a_w_s[l], a_b_s[l])
        y_b = neighbourhood_attention(q.reshape(B, N, NA_HEADS, HEAD_DIM),
                                      k.reshape(B, N, NA_HEADS, HEAD_DIM),
                                      va.reshape(B, N, NA_HEADS, HEAD_DIM), kc, vc, na_rpb[l])
        x_new = x + gt1 * (jnp.concatenate([y_a, y_b], axis=-1) @ w_out[l])

        h2 = modulate(rmsnorm(x_new, g_norm2[l]), sh2, sc2)
        x_new = x_new + gt2 * conv_ffn(h2, w_up[l], conv_w[l], conv_b[l], w_down[l])

        if l + 1 < DEPTH:
            uc, vgc, qc = jnp.split(hc @ w_in[l][:, :KV_START], [A_WIDTH, 2 * A_WIDTH], axis=-1)
            yc_a = chunk_gmlp(uc, vgc, a_ln_g[l], a_ln_b[l], a_w_s[l], a_b_s[l])
            yc_b = context_attention(qc.reshape(B, C, NA_HEADS, HEAD_DIM), kc, vc)
            ctx = ctx + cgt1 * (jnp.concatenate([yc_a, yc_b], axis=-1) @ w_out[l])
            hc2 = modulate(rmsnorm(ctx, g_norm2[l]), csh2, csc2)
            ctx = ctx + cgt2 * conv_ffn(hc2, w_up[l], conv_w[l], conv_b[l], w_down[l])
        x = x_new
    return rmsnorm(x, g_final)
```

```python
from contextlib import ExitStack
import numpy as np
import concourse.bass as bass
import concourse.mybir as mybir
from concourse.bass_utils import run_bass_kernel_spmd

F32 = mybir.dt.float32
BF16 = mybir.dt.bfloat16
AF = mybir.ActivationFunctionType
ALU = mybir.AluOpType

D = 4096
NKC = 32
DFF = 11008
NFB = 86
NCORES = 8
EPS = 1e-6
NEG = -30000.0
NPARTS = 4
PARTS = [(0, 22), (22, 22), (44, 21), (65, 21)]

KEYS = {-1: [-3, -2, -1, 0, 1], 0: [-2, -1, 0, 1, 2, 3], 1: [-1, 0, 1, 2, 3], 2: [0, 1, 2, 3, 4],
        3: [1, 2, 3, 4, 5], 4: [2, 3, 4, 5, 6], 5: [3, 4, 5, 6, 7], 6: [4, 5, 6, 7, 8],
        7: [4, 5, 6, 7, 8, 9], 8: [6, 7, 8, 9]}
QTILES = list(range(-1, 9))
PAIRS = [(jl, t) for jl in QTILES for t in KEYS[jl]]
PAIR_IDX = {p: i for i, p in enumerate(PAIRS)}
NPAIR = len(PAIRS)


def slot_of(t):
    if -1 <= t <= 8:
        return t + 1
    return {-3: 10, -2: 11, 9: 12}[t]


class Trk:
    def __init__(self, nc, es):
        self.nc = nc
        self.es = es
        self.E = {"pe": nc.tensor, "act": nc.scalar, "dve": nc.vector, "pool": nc.gpsimd, "sp": nc.sync}
        self.sem = {e: es.enter_context(nc.semaphore("c_" + e)) for e in self.E}
        self.cnt = {e: 0 for e in self.E}
        self.seen = {e: {} for e in self.E}
        self.res = {}
        self.dsem = {}
        self.pending = {}

    def _wait(self, eng, tok):
        key, sem, val = tok
        if self.seen[eng].get(key, 0) >= val:
            return
        self.E[eng].wait_ge(sem, val)
        self.seen[eng][key] = val

    def _deps(self, eng, reads, writes):
        toks = []
        for r in reads:
            st = self.res.get(r)
            if st and st[0]:
                toks.append(st[0])
        for w in writes:
            st = self.res.get(w)
            if st:
                if st[0]:
                    toks.append(st[0])
                toks.extend(st[1].values())
        for t in toks:
            if eng == "pe" and t[0] == "pe":
                continue
            self._wait(eng, t)

    def _commit(self, tok, reads, writes):
        for r in reads:
            st = self.res.setdefault(r, [None, {}])
            st[1][tok[0]] = tok
        for w in writes:
            self.res[w] = [tok, {}]

    def op(self, eng, fn, reads=(), writes=()):
        self._deps(eng, reads, writes)
        ins = fn()
        self.cnt[eng] += 1
        ins.then_inc(self.sem[eng], 1)
        self._commit((eng, self.sem[eng], self.cnt[eng]), reads, writes)

    def dma(self, eng, slot, out, in_, reads=(), writes=()):
        self._deps(eng, reads, writes)
        if slot not in self.dsem:
            self.dsem[slot] = [self.es.enter_context(self.nc.semaphore("d_" + slot)), 0]
        s = self.dsem[slot]
        ins = self.E[eng].dma_start(out=out, in_=in_)
        s[1] += 16
        ins.then_inc(s[0], 16)
        tok = ("d_" + slot, s[0], s[1])
        self._commit(tok, reads, writes)
        self.pending[tok[0]] = tok

    def barrier(self):
        for e in self.E:
            if e != "sp" and self.cnt[e] > 0:
                self._wait("sp", (e, self.sem[e], self.cnt[e]))
        for tok in self.pending.values():
            self._wait("sp", tok)
        self.pending = {}
        ins = self.E["sp"].sem_inc(self.sem["sp"], 1)
        self.cnt["sp"] += 1
        tok = ("sp", self.sem["sp"], self.cnt["sp"])
        for e in self.E:
            if e != "sp":
                self._wait(e, tok)
        self.res = {}


def build_nc(dbg=None):
    nc = bass.Bass("TRN2", target_bir_lowering=False)
    es = ExitStack()
    _build(nc, es, dbg or {})
    es.close()
    return nc


def _build(nc, es, dbg):
    stop_after = dbg.get("stop_after")

    in_names = []

    def inp(name, shape, dt=F32):
        in_names.append(name)
        return nc.dram_tensor(name, list(shape), dt, kind="ExternalInput").ap()
    nc._in_names = in_names

    xa = inp("xa", [1664, D])
    ctx_in = inp("ctx", [256, D])
    c2 = inp("c2", [2, D])
    w_ada = inp("w_ada", [D, 6 * D])
    b_ada = inp("b_ada", [6 * D])
    g1 = inp("g_norm1", [D])
    ident_in = inp("ident", [128, 128])
    flags_in = inp("flags", [128, 2])
    g2 = inp("g_norm2", [D])
    out = nc.dram_tensor("out", [1024, D], F32, kind="ExternalOutput").ap()

    ycat_d = nc.dram_tensor("ycat_d", [32, 128, 1026], BF16).ap()
    xnew_d = nc.dram_tensor("xnew_d", [1026, D], F32).ap()
    part_d = nc.dram_tensor("part_d", [32, 128, 1024], F32).ap()
    y_d = nc.dram_tensor("y_d", [1024, D], F32).ap()

    T = Trk(nc, es)
    E = T.E
    dbg_outs = []

    def sb(st, name, shape, dt):
        return st.enter_context(nc.sbuf_tensor("s_" + name, list(shape), dt))

    def dump(name, ap, shape, dt):
        if name not in dbg.get("dump", ()):
            return
        d = nc.dram_tensor("dbg_" + name, list(shape), dt, kind="ExternalOutput").ap()
        T.barrier()
        T.dma("sp", "dbg_" + name, out=d, in_=ap)
        T.barrier()
        dbg_outs.append("dbg_" + name)

    pb = [es.enter_context(nc.psum_tensor("pb%d" % i, [128, 512], F32)) for i in range(4)]
    pS = [es.enter_context(nc.psum_tensor("pS%d" % i, [128, 1024], F32)) for i in range(2)]

    ident = sb(es, "ident", [128, 128], F32)
    vec = sb(es, "vec", [128, 12, 32], F32)
    gfb = sb(es, "gfb", [128, 32], F32)
    flags = sb(es, "flags", [128, 2], F32)
    bada = sb(es, "bada", [128, 192], F32)
    scT = sb(es, "scT", [128, 32, 2], BF16)
    NWB = 4
    wbuf = [sb(es, "wbuf%d" % i, [128, 32, 128], BF16) for i in range(NWB)]
    V_SH1, V_SC1, V_GT1, V_SH2, V_SC2, V_GT2, V_CSH1, V_CSC1, V_A1, V_CA1, V_A2, V_G = range(12)

    T.dma("sp", "ident", out=ident[:], in_=ident_in[:, :], writes=["ident"])
    T.dma("sp", "flags", out=flags[:], in_=flags_in[:, :], writes=["flags"])

    class WS:
        def __init__(self):
            self.blocks = []
            self.issued = 0

        def add(self, ap, nk):
            self.blocks.append((ap, nk))
            return len(self.blocks) - 1

        def get(self, i, ahead=NWB - 1):
            while self.issued < len(self.blocks) and self.issued <= i + ahead:
                j = self.issued
                ap, nk = self.blocks[j]
                b = j % NWB
                T.dma("pool", "wbuf%d" % b, out=wbuf[b][:, 0:nk, :], in_=ap, writes=["wbuf%d" % b])
                self.issued += 1
            return wbuf[i % NWB], "wbuf%d" % (i % NWB)

    ws = WS()

    def wblock(w_ap, c0, nk=NKC, k0=0):
        v = w_ap.rearrange("(kc p) c -> p kc c", p=128)
        return ws.add(v[:, k0:k0 + nk, c0:c0 + 128], nk)

    pbrot = [0]

    def linear(bi, chunks, evac, banks=(0, 1, 2)):
        wb, wkey = ws.get(bi)
        nk = ws.blocks[bi][1]
        for ci, (src, skey, c0, n) in enumerate(chunks):
            b = banks[pbrot[0] % len(banks)]
            pbrot[0] += 1
            bank = pb[b]
            bkey = "pb%d" % b

            def mm():
                last = None
                for k in range(nk):
                    last = E["pe"].matmul(bank[:, 0:n], lhsT=wb[:, k, :], rhs=src[:, k, c0:c0 + n],
                                          start=(k == 0), stop=(k == nk - 1))
                return last
            T.op("pe", mm, reads=[wkey, skey], writes=[bkey])
            evac(ci, bank, bkey, n)

    with ExitStack() as s0:
        scf = sb(s0, "scf", [128, 32, 2], F32)
        with nc.allow_non_contiguous_dma(reason="small vector layouts"):
            for j in range(2):
                T.dma("sp", "scf", out=scf[:, :, j], in_=c2[j].rearrange("(kc p) -> p kc", p=128), writes=["scf"])
            T.dma("sp", "bada", out=bada[:], in_=b_ada.rearrange("(b p) -> p b", p=128), writes=["bada"])
            T.dma("sp", "vecg", out=vec[:, V_G, :], in_=g1.rearrange("(b p) -> p b", p=128), writes=["vecg"])
            T.dma("sp", "gfb", out=gfb[:], in_=g2.rearrange("(b p) -> p b", p=128), writes=["gfb"])
        T.op("act", lambda: E["act"].activation(out=scT[:], in_=scf[:], func=AF.Silu), reads=["scf"], writes=["scT"])
        T.barrier()

    def ada_group(blocks, cb0, ps, pkey, ctx_too=False, evac=True):
        n = len(blocks)
        for j, bi in enumerate(blocks):
            wb, wkey = ws.get(bi)

            def mm():
                last = None
                for k in range(NKC):
                    last = E["pe"].matmul(ps[:, 2 * j:2 * j + 2], lhsT=wb[:, k, :], rhs=scT[:, k, :],
                                          start=(k == 0), stop=(k == NKC - 1))
                return last
            T.op("pe", mm, reads=[wkey, "scT"], writes=[pkey])
        if dbg.get("no_evac") or not evac:
            return
        v = cb0 // 32
        blk0 = cb0 % 32
        psv = ps[:, 0:2 * n].rearrange("p (b j) -> p b j", j=2)
        T.op("dve", lambda: E["dve"].tensor_tensor(out=vec[:, v, blk0:blk0 + n], in0=psv[:, :, 0],
                                                  in1=bada[:, cb0:cb0 + n], op=ALU.add),
             reads=[pkey, "bada"], writes=["vec%d_%d" % (v, blk0)])
        if ctx_too:
            T.op("dve", lambda: E["dve"].tensor_tensor(out=vec[:, V_CSH1 + v, blk0:blk0 + n], in0=psv[:, :, 1],
                                                      in1=bada[:, cb0:cb0 + n], op=ALU.add),
                 reads=[pkey, "bada"], writes=["vecc%d_%d" % (v, blk0)])

    w_in = inp("w_in", [D, 10240])
    ln_g = inp("a_ln_g", [2048])
    ln_b = inp("a_ln_b", [2048])
    wsT_in = inp("wsT", [128, 16, 128])
    b_s = inp("a_b_s", [2048])
    tab = inp("tab", [16, 128, 7 * 128])
    rmask_in = inp("rmask", [2, NPAIR * 128])
    rowsel_in = inp("rowsel", [2, 128])

    def norm_tiles(st, tiles, tagp, raw=False, between=None, nbuf=2):
        xt = [sb(st, tagp + "xt%d" % i, [128, D], F32) for i in range(nbuf)]
        junk = sb(st, tagp + "junk", [128, D], BF16)
        stat = sb(st, tagp + "stat", [128, 4 * len(tiles)], F32)
        T.op("dve", lambda: E["dve"].memset(stat[:], 0.0), writes=[tagp + "stat"])
        for i, (src, n, dst_fn, dkey, ai, bi) in enumerate(tiles):
            x_t = xt[i % nbuf]
            xk = tagp + "xt%d" % (i % nbuf)
            dkey0 = dkey
            T.dma("sp", xk, out=x_t[0:n, :], in_=src, writes=[xk])
            ss = stat[0:n, 4 * i:4 * i + 1]
            ms = stat[0:n, 4 * i + 1:4 * i + 2]
            sd = stat[0:n, 4 * i + 2:4 * i + 3]
            rs = stat[0:n, 4 * i + 3:4 * i + 4]
            sk = tagp + "st%d" % i
            T.op("act", lambda: E["act"].activation(out=junk[0:n, :], in_=x_t[0:n, :], func=AF.Square, accum_out=ss),
                 reads=[xk, tagp + "stat"], writes=[tagp + "junk", sk])
            T.op("dve", lambda: E["dve"].tensor_scalar(out=ms, in0=ss, scalar1=1.0 / D, scalar2=EPS,
                                                      op0=ALU.mult, op1=ALU.add), reads=[sk], writes=[sk + "m"])
            T.op("act", lambda: E["act"].activation(out=sd, in_=ms, func=AF.Sqrt), reads=[sk + "m"], writes=[sk + "s"])
            T.op("dve", lambda: E["dve"].reciprocal(out=rs, in_=sd), reads=[sk + "s"], writes=[sk + "r"])
            T.op("dve", lambda: E["dve"].tensor_scalar(out=x_t[0:n, :], in0=x_t[0:n, :], scalar1=rs, scalar2=None,
                                                      op0=ALU.mult), reads=[sk + "r", xk], writes=[xk])
            for gq in range(8):
                b = gq % 4
                bank = pb[b]
                bkey = "pb%d" % b

                def tr():
                    last = None
                    for j in range(4):
                        blk = gq * 4 + j
                        last = E["pe"].transpose(bank[:, j * n:(j + 1) * n], x_t[0:n, blk * 128:(blk + 1) * 128],
                                                 ident[0:n, 0:n])
                    return last
                T.op("pe", tr, reads=[xk, "ident"], writes=[bkey])
                for j in range(4):
                    blk = gq * 4 + j
                    dst = dst_fn(blk)
                    dkey = dkey0 + "_%d_%d" % (i, blk)
                    if raw:
                        if gq % 2 == 0:
                            T.op("act", lambda: E["act"].copy(out=dst, in_=bank[:, j * n:(j + 1) * n]), reads=[bkey], writes=[dkey])
                        else:
                            T.op("dve", lambda: E["dve"].tensor_copy(out=dst, in_=bank[:, j * n:(j + 1) * n]), reads=[bkey], writes=[dkey])
                    elif gq % 2 == 0:
                        T.op("act", lambda: E["act"].activation(out=dst, in_=bank[:, j * n:(j + 1) * n], func=AF.Identity,
                                                                bias=vec[:, bi, blk:blk + 1], scale=vec[:, ai, blk:blk + 1]),
                             reads=[bkey], writes=[dkey])
                    else:
                        T.op("dve", lambda: E["dve"].tensor_scalar(out=dst, in0=bank[:, j * n:(j + 1) * n],
                                                                  scalar1=vec[:, ai, blk:blk + 1],
                                                                  scalar2=vec[:, bi, blk:blk + 1],
                                                                  op0=ALU.mult, op1=ALU.add),
                             reads=[bkey], writes=[dkey])
            if between is not None:
                between(i)

    with ExitStack() as s12:
        hTm = sb(s12, "hTm", [128, NKC, 1280], BF16)
        with ExitStack() as sat:
            hTh = sb(sat, "hTh", [128, NKC, 384], BF16)
            hTc = sb(sat, "hTc", [128, NKC, 256], BF16)
            with ExitStack() as s1:
                tiles = []
                for slot in range(13):
                    if slot < 10:
                        f = (lambda blk, slot=slot: hTm[:, blk, slot * 128:(slot + 1) * 128])
                        key = "hTm"
                    else:
                        f = (lambda blk, slot=slot: hTh[:, blk, (slot - 10) * 128:(slot - 9) * 128])
                        key = "hTh"
                    tiles.append((xa[slot * 128:(slot + 1) * 128, :], 128, f, key, V_A1, V_SH1))
                for ct in range(2):
                    f = (lambda blk, ct=ct: hTc[:, blk, ct * 128:(ct + 1) * 128])
                    tiles.append((ctx_in[ct * 128:(ct + 1) * 128, :], 128, f, "hTc", V_CA1, V_CSH1))
                ada0 = [wblock(w_ada, cb * 128) for cb in range(64)]

                def between(i):
                    lo, hi = (i * 16) // 15, ((i + 1) * 16) // 15
                    for g_ in range(lo, hi):
                        if dbg.get("bar_int"):
                            T.barrier()
                        ada_group(ada0[4 * g_:4 * g_ + 4], 4 * g_, pS[g_ % 2][:, 8 * g_:8 * g_ + 8], "pSa_%d" % g_, ctx_too=True)
                if dbg.get("no_interleave"):
                    norm_tiles(s1, tiles, "n1", raw=True)
                    for i_ in range(15):
                        between(i_)
                else:
                    norm_tiles(s1, tiles, "n1", raw=True, between=between)
                T.barrier()
                T.op("dve", lambda: E["dve"].scalar_tensor_tensor(out=vec[:, V_A1, :], in0=vec[:, V_SC1, :], scalar=1.0,
                                                                  in1=vec[:, V_G, :], op0=ALU.add, op1=ALU.mult), writes=["vecA1"])
                T.op("dve", lambda: E["dve"].scalar_tensor_tensor(out=vec[:, V_CA1, :], in0=vec[:, V_CSC1, :], scalar=1.0,
                                                                  in1=vec[:, V_G, :], op0=ALU.add, op1=ALU.mult), writes=["vecA1"])
                k_ = 0
                for (ht, hk, ai, bi) in ((hTm, "hTm", V_A1, V_SH1), (hTh, "hTh", V_A1, V_SH1), (hTc, "hTc", V_CA1, V_CSH1)):
                    for blk in range(NKC):
                        if k_ % 2 == 0:
                            T.op("act", lambda: E["act"].activation(out=ht[:, blk, :], in_=ht[:, blk, :], func=AF.Identity,
                                                                    bias=vec[:, bi, blk:blk + 1], scale=vec[:, ai, blk:blk + 1]),
                                 reads=["vecA1"], writes=[hk + "%d" % blk])
                        else:
                            T.op("dve", lambda: E["dve"].tensor_scalar(out=ht[:, blk, :], in0=ht[:, blk, :],
                                                                      scalar1=vec[:, ai, blk:blk + 1], scalar2=vec[:, bi, blk:blk + 1],
                                                                      op0=ALU.mult, op1=ALU.add),
                                 reads=["vecA1"], writes=[hk + "%d" % blk])
                        k_ += 1
                T.barrier()
            dump("hTm", hTm[:], [128, NKC, 1280], BF16)
            if stop_after == "S1":
                return finish(nc, es, T, dbg_outs)

            QTs = [sb(sat, "QT%d" % i, [128, 1280], BF16) for i in range(2)]
            KTs = [sb(sat, "KT%d" % i, [128, 1920], BF16) for i in range(2)]
            Vt = sb(sat, "Vt", [128, 15, 130], BF16)
            Th = [sb(sat, "Th%d" % i, [128, 7 * 128], F32) for i in range(2)]
            ssb = [sb(sat, "ssb%d" % i, [128, 768], F32) for i in range(2)]
            Pt = [sb(sat, "Pt%d" % i, [128, 1024], BF16) for i in range(2)]
            ybtm = [sb(sat, "ybtm%d" % i, [128, 128], F32) for i in range(2)]
            rden = sb(sat, "rden", [128, 16], F32)
            ybT = [sb(sat, "ybT%d" % i, [128, 1280], BF16) for i in range(1)]
            rmask = sb(sat, "rmask", [2, NPAIR * 128], BF16)
            rowsel = sb(sat, "rowsel", [2, 128], BF16)
            T.dma("pool", "rmask", out=rmask[:], in_=rmask_in[:, :], writes=["rmask"])
            T.dma("pool", "rowsel", out=rowsel[:], in_=rowsel_in[:, :], writes=["rowsel"])
            T.op("dve", lambda: E["dve"].memset(Vt[:], 1.0), writes=["Vt"])

            main_chunks = [(hTm, "hTm", 0, 512), (hTm, "hTm", 512, 512), (hTm, "hTm", 1024, 256)]
            kv_chunks = main_chunks + [(hTh, "hTh", 0, 384), (hTc, "hTc", 0, 256)]
            kv_off = [0, 512, 1024, 1280, 1664]
            q_chunks = [(hTm, "hTm", 127, 512), (hTm, "hTm", 639, 512), (hTm, "hTm", 1151, 2)]
            qscale = 128.0 ** -0.5
            blk_q = [None] * 16
            blk_k = [None] * 16
            blk_v = [None] * 16
            att_ada = [None] * 16
            blk_q[0] = wblock(w_in, 4096)
            blk_k[0] = wblock(w_in, 6144)
            for h in range(16):
                blk_v[h] = wblock(w_in, 8192 + 128 * h)
                ada_h = [None] * 6
                if h + 1 < 16:
                    blk_q[h + 1] = wblock(w_in, 4096 + 128 * (h + 1))
                ada_h[0] = wblock(w_ada, (64 + 6 * h + 0) * 128)
                ada_h[1] = wblock(w_ada, (64 + 6 * h + 1) * 128)
                if h + 1 < 16:
                    blk_k[h + 1] = wblock(w_in, 6144 + 128 * (h + 1))
                for j in range(2, 6):
                    ada_h[j] = wblock(w_ada, (64 + 6 * h + j) * 128)
                att_ada[h] = ada_h

            ipb = [0]

            def proj_item(hn, kind):
                QTn, KTn = QTs[hn % 2], KTs[hn % 2]
                qk_, kk_ = "QT%d" % (hn % 2), "KT%d" % (hn % 2)

                def item():
                    wb, wkey = ws.get(blk_q[hn] if kind == "q" else blk_k[hn])
                    chunks = q_chunks if kind == "q" else kv_chunks
                    for ci, (src, skey, c0, n) in enumerate(chunks):
                        b = ipb[0] % 2
                        ipb[0] += 1
                        bank = pb[b]
                        bkey = "pb%d" % b

                        def mm():
                            last = None
                            for k in range(NKC):
                                last = E["pe"].matmul(bank[:, 0:n], lhsT=wb[:, k, :], rhs=src[:, k, c0:c0 + n],
                                                      start=(k == 0), stop=(k == NKC - 1))
                            return last
                        T.op("pe", mm, reads=[wkey, skey], writes=[bkey])
                        if kind == "q":
                            T.op("act", lambda: E["act"].activation(out=QTn[:, c0:c0 + n], in_=bank[:, 0:n], func=AF.Identity, scale=qscale),
                                 reads=[bkey], writes=[qk_])
                        else:
                            o0 = kv_off[ci]
                            T.op("dve", lambda: E["dve"].tensor_copy(out=KTn[:, o0:o0 + n], in_=bank[:, 0:n]), reads=[bkey], writes=[kk_])
                return item

            def ada_item(h_, j_):
                def item():
                    ada_group([att_ada[h_][j_]], 64 + 6 * h_ + j_, pb[2][:, 12 * h_ + 2 * j_:12 * h_ + 2 * j_ + 2], "pb2", evac=False)
                return item

            def head_items(h_):
                items = []
                if h_ + 1 < 16:
                    items.append(proj_item(h_ + 1, "q"))
                items.append(ada_item(h_, 0))
                items.append(ada_item(h_, 1))
                if h_ + 1 < 16:
                    items.append(proj_item(h_ + 1, "k"))
                for j_ in range(2, 6):
                    items.append(ada_item(h_, j_))
                return items

            proj_item(0, "q")()
            proj_item(0, "k")()


            def v_src(s_):
                if s_ < 10:
                    return hTm, "hTm", s_ * 128
                if s_ < 13:
                    return hTh, "hTh", (s_ - 10) * 128
                return hTc, "hTc", (s_ - 13) * 128

            for h in range(16):
                QT, KT = QTs[h % 2], KTs[h % 2]
                QTk, KTk = "QT%d" % (h % 2), "KT%d" % (h % 2)
                th = Th[h % 2]
                thk = "Th%d" % (h % 2)
                T.dma("sp", thk, out=th[:], in_=tab[h], writes=[thk])
                wv, wvk = ws.get(blk_v[h])
                for g4 in range(4):
                    nt = min(4, 15 - 4 * g4)
                    b = ipb[0] % 2
                    ipb[0] += 1
                    bank = pb[b]
                    bkey = "pb%d" % b

                    def mv():
                        last = None
                        for j in range(nt):
                            src, skey, c0 = v_src(4 * g4 + j)
                            for k in range(NKC):
                                last = E["pe"].matmul(bank[:, j * 128:(j + 1) * 128], lhsT=src[:, k, c0:c0 + 128], rhs=wv[:, k, :],
                                                      start=(k == 0), stop=(k == NKC - 1))
                        return last
                    T.op("pe", mv, reads=[wvk, "hTm", "hTh", "hTc"], writes=[bkey])
                    T.op("dve", lambda: E["dve"].tensor_copy(out=Vt[:, 4 * g4:4 * g4 + nt, 0:128],
                                                            in_=bank[:, 0:nt * 128].rearrange("p (s d) -> p s d", d=128)),
                         reads=[bkey], writes=["Vt"])

                nxt = head_items(h)
                yb = ybT[0]
                ybk = "ybT0"

                def qk(qi):
                    jl = QTILES[qi]
                    keys = KEYS[jl]
                    nl = len(keys)
                    ps = pS[qi % 2]

                    def f():
                        last = None
                        for i, t in enumerate(keys):
                            s_ = slot_of(t)
                            E["pe"].matmul(ps[:, i * 128:(i + 1) * 128], lhsT=KT[:, s_ * 128:(s_ + 1) * 128],
                                           rhs=QT[:, qi * 128:(qi + 1) * 128], start=True, stop=False)
                            pi = PAIR_IDX[(jl, t)]
                            last = E["pe"].matmul(ps[:, i * 128:(i + 1) * 128], lhsT=rowsel[:, :],
                                                  rhs=rmask[:, pi * 128:(pi + 1) * 128], start=False, stop=True)
                        for c in range(2):
                            i = nl + c
                            s_ = 13 + c
                            last = E["pe"].matmul(ps[:, i * 128:(i + 1) * 128], lhsT=KT[:, s_ * 128:(s_ + 1) * 128],
                                                  rhs=QT[:, qi * 128:(qi + 1) * 128], start=True, stop=True)
                        return last
                    T.op("pe", f, reads=[KTk, QTk, "rowsel", "rmask"], writes=["pS%d" % (qi % 2)])

                qk(0)
                for qi in range(10):
                    jl = QTILES[qi]
                    keys = KEYS[jl]
                    nl = len(keys)
                    d0 = keys[0] - jl
                    ps = pS[qi % 2]
                    psk = "pS%d" % (qi % 2)
                    s_sb = ssb[qi % 2]
                    ssk = "ssb%d" % (qi % 2)
                    P = Pt[qi % 2]
                    Pk = "Pt%d" % (qi % 2)
                    if qi + 1 < 10:
                        qk(qi + 1)
                    T.op("dve", lambda: E["dve"].tensor_tensor(out=s_sb[:, 0:nl * 128], in0=ps[:, 0:nl * 128],
                                                              in1=th[:, (d0 + 3) * 128:(d0 + 3 + nl) * 128], op=ALU.add),
                         reads=[psk, thk], writes=[ssk])
                    T.op("act", lambda: E["act"].activation(out=P[:, 0:nl * 128], in_=s_sb[:, 0:nl * 128], func=AF.Exp),
                         reads=[ssk], writes=[Pk])
                    T.op("act", lambda: E["act"].activation(out=P[:, nl * 128:(nl + 2) * 128],
                                                            in_=ps[:, nl * 128:(nl + 2) * 128], func=AF.Exp),
                         reads=[psk], writes=[Pk])
                    if nxt:
                        nxt.pop(0)()

                    def pv():
                        last = None
                        sl = [slot_of(t) for t in keys] + [13, 14]
                        for i, s_ in enumerate(sl):
                            last = E["pe"].matmul(pb[3][:, 0:129], lhsT=P[:, i * 128:(i + 1) * 128], rhs=Vt[:, s_, 0:129],
                                                  start=(i == 0), stop=(i == len(sl) - 1))
                        return last
                    T.op("pe", pv, reads=[Pk, "Vt"], writes=["pb3a"])
                    rd = rden[:, qi:qi + 1]
                    T.op("dve", lambda: E["dve"].reciprocal(out=rd, in_=pb[3][:, 128:129]), reads=["pb3a"], writes=["rden%d" % qi])
                    ytm = ybtm[qi % 2]
                    ytk = "ybtm%d" % (qi % 2)
                    T.op("act", lambda: E["act"].activation(out=ytm[:], in_=pb[3][:, 0:128], func=AF.Identity, scale=rd),
                         reads=["pb3a", "rden%d" % qi], writes=[ytk])
                    T.op("pe", lambda: E["pe"].transpose(pb[3][:, 256:384], ytm[:], ident[:]), reads=[ytk, "ident"], writes=["pb3b"])
                    T.op("dve", lambda: E["dve"].tensor_copy(out=yb[:, qi * 128:(qi + 1) * 128], in_=pb[3][:, 256:384]),
                         reads=["pb3b"], writes=[ybk])
                while nxt:
                    nxt.pop(0)()
                T.dma("sp", ybk, out=ycat_d[16 + h], in_=yb[:, 127:1153], reads=[ybk], writes=["ycat_d"])
                if h == 0:
                    dump("ybT0", yb[:], [128, 1280], BF16)
                    dump("KT0", KT[:], [128, 1920], BF16)
                    dump("QT0", QT[:], [128, 1280], BF16)
                    if stop_after == "H0":
                        return finish(nc, es, T, dbg_outs)
            T.barrier()

        with ExitStack() as sg:
            gvtm = sb(sg, "gvtm", [128, 10, 2048], BF16)
            gtmp = [sb(sg, "gtmp%d" % i, [128, 1280], F32) for i in range(2)]
            wsT = sb(sg, "wsT", [128, 16, 128], BF16)
            ones = sb(sg, "ones", [128, 128], BF16)
            Ct = sb(sg, "Ct", [128, 16, 128], F32)
            bsb = sb(sg, "bsb", [128, 2048], F32)
            lng = sb(sg, "lng", [128, 16], F32)
            lnb = sb(sg, "lnb", [128, 16], F32)
            gst = sb(sg, "gst", [128, 80], F32)
            uT = [sb(sg, "uT%d" % i, [128, 1280], BF16) for i in range(2)]
            yaT = [sb(sg, "yaT%d" % i, [128, 1280], BF16) for i in range(2)]
            stmp = [sb(sg, "stmp%d" % i, [128, 512], F32) for i in range(2)]
            gjunk = sb(sg, "gjunk", [128, 2048], BF16)
            T.dma("pool", "wsT", out=wsT[:], in_=wsT_in[:, :, :], writes=["wsT"])
            T.dma("sp", "bsb", out=bsb[:], in_=b_s.partition_broadcast(128), writes=["bsb"])
            with nc.allow_non_contiguous_dma(reason="small vector layouts"):
                T.dma("sp", "lng", out=lng[:], in_=ln_g.rearrange("(b p) -> p b", p=128), writes=["lng"])
                T.dma("sp", "lnb", out=lnb[:], in_=ln_b.rearrange("(b p) -> p b", p=128), writes=["lnb"])
            T.op("dve", lambda: E["dve"].memset(ones[:], 1.0), writes=["ones"])
            T.op("dve", lambda: E["dve"].memset(gst[:], 0.0), writes=["gst"])
            for g4 in range(4):
                bank = pb[g4 % 2]
                bkey = "pb%d" % (g4 % 2)

                def rsum():
                    last = None
                    for j in range(4):
                        g = g4 * 4 + j
                        last = E["pe"].matmul(bank[:, j * 128:(j + 1) * 128], lhsT=ones[:], rhs=wsT[:, g, :], start=True, stop=True)
                    return last
                T.op("pe", rsum, reads=["ones", "wsT"], writes=[bkey])
                for j in range(4):
                    g = g4 * 4 + j
                    T.op("dve", lambda: E["dve"].scalar_tensor_tensor(out=Ct[:, g, :], in0=bank[:, j * 128:(j + 1) * 128],
                                                                      scalar=lnb[:, g:g + 1], in1=bsb[:, g * 128:(g + 1) * 128],
                                                                      op0=ALU.mult, op1=ALU.add),
                         reads=[bkey, "lnb", "bsb"], writes=["Ct"])
            v_blocks = [wblock(w_in, 2048 + 128 * g) for g in range(16)]
            u_blocks = []
            g_ada = []
            for g in range(16):
                u_blocks.append(wblock(w_in, 128 * g))
                g_ada.append([])
            main_chunks = [(hTm, "hTm", 0, 512), (hTm, "hTm", 512, 512), (hTm, "hTm", 1024, 256)]
            def v_lin(g):
                gt = gtmp[g % 2]
                gtk = "gtmp%d" % (g % 2)

                def evg(ci, bank, bkey, n):
                    c0 = main_chunks[ci][2]
                    T.op("act", lambda: E["act"].activation(out=gt[:, c0:c0 + n], in_=bank[:, 0:n], func=AF.Gelu_apprx_tanh),
                         reads=[bkey], writes=[gtk])
                linear(v_blocks[g], main_chunks, evg, banks=(0, 1))

            def v_tr(g):
                gt = gtmp[g % 2]
                gtk = "gtmp%d" % (g % 2)
                for g4 in range(3):
                    nt = min(4, 10 - 4 * g4)
                    bank, bkey = ((pb[3], "pb3"), (pS[1], "pS1"))[g4 % 2]

                    def trg():
                        last = None
                        for j in range(nt):
                            ch = 4 * g4 + j
                            last = E["pe"].transpose(bank[:, j * 128:(j + 1) * 128], gt[:, ch * 128:(ch + 1) * 128], ident[:])
                        return last
                    T.op("pe", trg, reads=[gtk, "ident"], writes=[bkey])
                    T.op("dve", lambda: E["dve"].tensor_copy(out=gvtm[:, 4 * g4:4 * g4 + nt, g * 128:(g + 1) * 128],
                                                            in_=bank[:, 0:nt * 128].rearrange("p (s d) -> p s d", d=128)),
                         reads=[bkey], writes=["gvtm"])

            v_lin(0)
            for g in range(16):
                if g + 1 < 16:
                    v_lin(g + 1)
                v_tr(g)
            for ch in range(10):
                T.op("act", lambda: E["act"].activation(out=gjunk[:], in_=gvtm[:, ch, :], func=AF.Identity,
                                                        accum_out=gst[:, ch:ch + 1]), reads=["gvtm", "gst"], writes=["gjunk", "gs%d" % ch])
                T.op("act", lambda: E["act"].activation(out=gjunk[:], in_=gvtm[:, ch, :], func=AF.Square,
                                                        accum_out=gst[:, 10 + ch:11 + ch]), reads=["gvtm", "gst"], writes=["gjunk", "gq%d" % ch])
            allst = ["gs%d" % ch for ch in range(10)] + ["gq%d" % ch for ch in range(10)]
            mean = gst[:, 20:30]
            var = gst[:, 30:40]
            msq = gst[:, 40:50]
            sdv = gst[:, 50:60]
            rstd = gst[:, 60:70]
            nb = gst[:, 70:80]
            T.op("dve", lambda: E["dve"].tensor_scalar(out=mean, in0=gst[:, 0:10], scalar1=1.0 / 2048, scalar2=None, op0=ALU.mult),
                 reads=allst, writes=["g_mean"])
            T.op("dve", lambda: E["dve"].tensor_tensor(out=msq, in0=mean, in1=mean, op=ALU.mult), reads=["g_mean"], writes=["g_msq"])
            T.op("dve", lambda: E["dve"].scalar_tensor_tensor(out=var, in0=gst[:, 10:20], scalar=1.0 / 2048, in1=msq,
                                                              op0=ALU.mult, op1=ALU.subtract), reads=allst + ["g_msq"], writes=["g_var"])
            T.op("dve", lambda: E["dve"].tensor_scalar(out=var, in0=var, scalar1=EPS, scalar2=None, op0=ALU.add),
                 reads=["g_var"], writes=["g_var"])
            T.op("act", lambda: E["act"].activation(out=sdv, in_=var, func=AF.Sqrt), reads=["g_var"], writes=["g_sd"])
            T.op("dve", lambda: E["dve"].reciprocal(out=rstd, in_=sdv), reads=["g_sd"], writes=["g_rstd"])
            T.op("dve", lambda: E["dve"].scalar_tensor_tensor(out=nb, in0=mean, scalar=-1.0, in1=rstd, op0=ALU.mult, op1=ALU.mult),
                 reads=["g_mean", "g_rstd"], writes=["g_nb"])
            for ch in range(10):
                T.op("act", lambda: E["act"].activation(out=gvtm[:, ch, :], in_=gvtm[:, ch, :], func=AF.Identity,
                                                        bias=gst[:, 70 + ch:71 + ch], scale=gst[:, 60 + ch:61 + ch]),
                     reads=["gvtm", "g_nb", "g_rstd"], writes=["gvtm"])
            dump("gvtm", gvtm[:], [128, 10, 2048], BF16)
            u_chunks = [(hTm, "hTm", 127, 512), (hTm, "hTm", 639, 512), (hTm, "hTm", 1151, 2)]
            for g in range(16):
                u = uT[g % 2]
                uk = "uT%d" % (g % 2)
                ya = yaT[g % 2]
                yak = "yaT%d" % (g % 2)

                def evu(ci, bank, bkey, n):
                    c0 = u_chunks[ci][2]
                    T.op("act", lambda: E["act"].activation(out=u[:, c0:c0 + n], in_=bank[:, 0:n], func=AF.Gelu_apprx_tanh),
                         reads=[bkey], writes=[uk])
                linear(u_blocks[g], u_chunks, evu, banks=(0, 1))
                for g4 in range(3):
                    nt = min(4, 10 - 4 * g4)
                    bank, bkey = ((pb[3], "pb3"), (pS[1], "pS1"))[g4 % 2]
                    st_ = stmp[g4 % 2]
                    stk = "stmp%d" % (g4 % 2)

                    def spm():
                        last = None
                        for j in range(nt):
                            ch = 4 * g4 + j
                            last = E["pe"].matmul(bank[:, j * 128:(j + 1) * 128], lhsT=gvtm[:, ch, g * 128:(g + 1) * 128],
                                                  rhs=wsT[:, g, :], start=True, stop=True)
                        return last
                    T.op("pe", spm, reads=["gvtm", "wsT"], writes=[bkey])
                    for j in range(nt):
                        T.op("dve", lambda: E["dve"].scalar_tensor_tensor(out=st_[:, j * 128:(j + 1) * 128],
                                                                          in0=bank[:, j * 128:(j + 1) * 128],
                                                                          scalar=lng[:, g:g + 1], in1=Ct[:, g, :],
                                                                          op0=ALU.mult, op1=ALU.add),
                             reads=[bkey, "lng", "Ct"], writes=[stk])
                    c0 = 4 * g4 * 128
                    T.op("dve", lambda: E["dve"].tensor_tensor(out=ya[:, c0:c0 + nt * 128], in0=st_[:, 0:nt * 128],
                                                              in1=u[:, c0:c0 + nt * 128], op=ALU.mult),
                         reads=[stk, uk], writes=[yak])
                T.dma("sp", yak, out=ycat_d[g], in_=ya[:, 127:1153], reads=[yak], writes=["ycat_d"])

                if g == 0:
                    dump("yaT0", ya[:], [128, 1280], BF16)
            T.barrier()
            psv2 = pb[2][:, 0:192].rearrange("p (b j) -> p b j", j=2)
            for v in range(2, 5):
                T.op("dve", lambda: E["dve"].tensor_tensor(out=vec[:, v, :], in0=psv2[:, (v - 2) * 32:(v - 1) * 32, 0],
                                                          in1=bada[:, v * 32:(v + 1) * 32], op=ALU.add), writes=["vec%d" % v])
            T.barrier()
    if stop_after == "S2":
        return finish(nc, es, T, dbg_outs)

    w_out = inp("w_out", [D, D])
    with ExitStack() as s3:
        ycT = sb(s3, "ycT", [128, 32, 1026], BF16)
        osb = [sb(s3, "osb%d" % i, [128, 1026], F32) for i in range(2)]
        xp = [sb(s3, "xp%d" % i, [128, 8, 128], F32) for i in range(2)]
        xo = [sb(s3, "xo%d" % i, [128, 8, 128], F32) for i in range(2)]
        xh2 = sb(s3, "xh2", [2, D], F32)
        xho = sb(s3, "xho", [2, D], F32)
        for q4 in range(4):
            T.dma("sp", "ycT", out=ycT[:, q4 * 8:(q4 + 1) * 8, :], in_=ycat_d[q4 * 8:(q4 + 1) * 8].rearrange("m p t -> p m t"),
                  reads=["ycat_d"], writes=["ycT"])
        T.dma("sp", "xh2", out=xh2[0:1, :], in_=xa[127:128, :], writes=["xh2"])
        T.dma("sp", "xh2", out=xh2[1:2, :], in_=xa[1152:1153, :], writes=["xh2"])
        o_blocks = [wblock(w_out, 128 * db) for db in range(32)]
        x_own = xa[128:1152, :].rearrange("(j p) d -> p j d", p=128)
        xn_own = xnew_d[1:1025, :].rearrange("(j p) d -> p j d", p=128)
        def s3_mm(db):
            wb, wkey = ws.get(o_blocks[db])
            ps = pS[db % 2]
            psk = "pS%d" % (db % 2)
            hb = pb[2 + db % 2]
            hbk = "pb%d" % (2 + db % 2)
            x_p = xp[db % 2]
            xpk = "xp%d" % (db % 2)
            T.dma("sp", xpk, out=x_p[:], in_=x_own[:, :, db * 128:(db + 1) * 128], writes=[xpk])

            def mm():
                last = None
                for hf in range(2):
                    for k in range(NKC):
                        last = E["pe"].matmul(ps[:, hf * 512:(hf + 1) * 512], lhsT=wb[:, k, :], rhs=ycT[:, k, 1 + hf * 512:1 + (hf + 1) * 512],
                                              start=(k == 0), stop=(k == NKC - 1))
                for k in range(NKC):
                    last = E["pe"].matmul(hb[:, 2 * db:2 * db + 2], lhsT=wb[:, k, :], rhs=ycT[:, k, 0:1026:1025],
                                          start=(k == 0), stop=(k == NKC - 1))
                return last
            T.op("pe", mm, reads=[wkey, "ycT"], writes=[psk, hbk])

        def s3_rest(db):
            ps = pS[db % 2]
            psk = "pS%d" % (db % 2)
            hb = pb[2 + db % 2]
            hbk = "pb%d" % (2 + db % 2)
            o = osb[db % 2]
            ok = "osb%d" % (db % 2)
            x_p = xp[db % 2]
            xpk = "xp%d" % (db % 2)
            x_o = xo[db % 2]
            xok = "xo%d" % (db % 2)
            gt1 = vec[:, V_GT1, db:db + 1]
            T.op("act", lambda: E["act"].activation(out=o[:, 1:1025], in_=ps[:, 0:1024], func=AF.Identity, scale=gt1),
                 reads=[psk], writes=[ok])
            T.op("act", lambda: E["act"].activation(out=o[:, 0:1026:1025], in_=hb[:, 2 * db:2 * db + 2], func=AF.Identity, scale=gt1),
                 reads=[hbk], writes=[ok])
            for g4 in range(2):
                bank = pb[g4]
                bkey = "pb%d" % g4

                def tr():
                    last = None
                    for j in range(4):
                        tj = 4 * g4 + j
                        last = E["pe"].transpose(bank[:, j * 128:(j + 1) * 128], o[:, 1 + tj * 128:1 + (tj + 1) * 128], ident[:])
                    return last
                T.op("pe", tr, reads=[ok, "ident"], writes=[bkey])
                T.op("dve", lambda: E["dve"].tensor_tensor(out=x_o[:, 4 * g4:4 * g4 + 4, :],
                                                          in0=bank[:, 0:512].rearrange("p (s d) -> p s d", d=128),
                                                          in1=x_p[:, 4 * g4:4 * g4 + 4, :], op=ALU.add),
                     reads=[bkey, xpk], writes=[xok])
            T.op("pe", lambda: E["pe"].transpose(hb[0:2, 256:384], o[:, 0:1026:1025], ident[:]), reads=[ok, "ident"], writes=[hbk])
            T.op("dve", lambda: E["dve"].tensor_tensor(out=xho[0:2, db * 128:(db + 1) * 128], in0=hb[0:2, 256:384],
                                                      in1=xh2[0:2, db * 128:(db + 1) * 128], op=ALU.add),
                 reads=[hbk, "xh2"], writes=["xho"])
            T.dma("sp", xok, out=xn_own[:, :, db * 128:(db + 1) * 128], in_=x_o[:], reads=[xok], writes=["xnew_d"])

        s3_mm(0)
        for db in range(32):
            if db + 1 < 32:
                s3_mm(db + 1)
            s3_rest(db)
        T.dma("sp", "xho", out=xnew_d[0:1, :], in_=xho[0:1, :], reads=["xho"], writes=["xnew_d"])
        T.dma("sp", "xho", out=xnew_d[1025:1026, :], in_=xho[1:2, :], reads=["xho"], writes=["xnew_d"])
        T.barrier()
    if "xnew" in dbg.get("dump", ()):
        dbg_outs.append("xnew_d")
    if stop_after == "S3":
        return finish(nc, es, T, dbg_outs)

    w_up = inp("w_up", [D, 2 * DFF])
    conv_w = inp("conv_w", [3, DFF])
    conv_b = inp("conv_b", [DFF])
    w_down = inp("w_down", [DFF, D])
    gf = inp("g_final", [D])
    with ExitStack() as s45:
        h2T = sb(s45, "h2T", [128, NKC, 1026], BF16)
        cw = sb(s45, "cw", [128, 3, NFB], F32)
        cb_ = sb(s45, "cb", [128, NFB], F32)
        with nc.allow_non_contiguous_dma(reason="small vector layouts"):
            T.dma("sp", "cw", out=cw[:], in_=conv_w.rearrange("t (b p) -> p t b", p=128), writes=["cw"])
            T.dma("sp", "cb", out=cb_[:], in_=conv_b.rearrange("(b p) -> p b", p=128), writes=["cb"])
        with ExitStack() as s4:
            T.op("dve", lambda: E["dve"].scalar_tensor_tensor(out=vec[:, V_A2, :], in0=vec[:, V_SC2, :], scalar=1.0,
                                                              in1=gfb[:], op0=ALU.add, op1=ALU.mult), writes=["vecA2"])
            T.barrier()
            tiles = []
            for j in range(8):
                f = (lambda blk, j=j: h2T[:, blk, 1 + j * 128:1 + (j + 1) * 128])
                tiles.append((xnew_d[1 + j * 128:1 + (j + 1) * 128, :], 128, f, "h2T", V_A2, V_SH2))
            norm_tiles(s4, tiles, "n2", nbuf=4)
            xt2 = sb(s4, "xt2", [2, D], F32)
            junk2 = sb(s4, "junk2", [2, D], BF16)
            st2 = sb(s4, "st2", [2, 4], F32)
            T.op("dve", lambda: E["dve"].memset(st2[:], 0.0), writes=["st2"])
            T.dma("sp", "xt2", out=xt2[0:1, :], in_=xnew_d[0:1, :], writes=["xt2"])
            T.dma("sp", "xt2", out=xt2[1:2, :], in_=xnew_d[1025:1026, :], writes=["xt2"])
            T.op("act", lambda: E["act"].activation(out=junk2[:], in_=xt2[:], func=AF.Square, accum_out=st2[:, 0:1]),
                 reads=["xt2", "st2"], writes=["junk2", "st2a"])
            T.op("dve", lambda: E["dve"].tensor_scalar(out=st2[:, 1:2], in0=st2[:, 0:1], scalar1=1.0 / D, scalar2=EPS,
                                                      op0=ALU.mult, op1=ALU.add), reads=["st2a"], writes=["st2b"])
            T.op("act", lambda: E["act"].activation(out=st2[:, 2:3], in_=st2[:, 1:2], func=AF.Sqrt), reads=["st2b"], writes=["st2c"])
            T.op("dve", lambda: E["dve"].reciprocal(out=st2[:, 3:4], in_=st2[:, 2:3]), reads=["st2c"], writes=["st2d"])
            T.op("dve", lambda: E["dve"].tensor_scalar(out=xt2[:], in0=xt2[:], scalar1=st2[:, 3:4], scalar2=None, op0=ALU.mult),
                 reads=["st2d", "xt2"], writes=["xt2"])
            for gq in range(8):
                bank = pb[gq % 4]
                bkey = "pb%d" % (gq % 4)

                def tr():
                    last = None
                    for j in range(4):
                        blk = gq * 4 + j
                        last = E["pe"].transpose(bank[:, j * 2:(j + 1) * 2], xt2[0:2, blk * 128:(blk + 1) * 128], ident[0:2, 0:2])
                    return last
                T.op("pe", tr, reads=["xt2", "ident"], writes=[bkey])
                for j in range(4):
                    blk = gq * 4 + j
                    T.op("dve", lambda: E["dve"].tensor_scalar(out=h2T[:, blk, 0:1026:1025], in0=bank[:, j * 2:(j + 1) * 2],
                                                              scalar1=vec[:, V_A2, blk:blk + 1], scalar2=vec[:, V_SH2, blk:blk + 1],
                                                              op0=ALU.mult, op1=ALU.add), reads=[bkey], writes=["h2T"])
            T.op("dve", lambda: E["dve"].tensor_scalar(out=h2T[:, :, 0], in0=h2T[:, :, 0], scalar1=flags[:, 0:1], scalar2=None, op0=ALU.mult),
                 reads=["h2T", "flags"], writes=["h2T"])
            T.op("dve", lambda: E["dve"].tensor_scalar(out=h2T[:, :, 1025], in0=h2T[:, :, 1025], scalar1=flags[:, 1:2], scalar2=None, op0=ALU.mult),
                 reads=["h2T", "flags"], writes=["h2T"])
            T.barrier()
        dump("h2T", h2T[:], [128, NKC, 1026], BF16)
        if stop_after == "S4":
            return finish(nc, es, T, dbg_outs)

        xn_own = xnew_d[1:1025, :].rearrange("(j p) d -> p j d", p=128)
        y_own = y_d.rearrange("(j p) d -> p j d", p=128)
        MAXF = max(n_ for _, n_ in PARTS)
        zT = sb(s45, "zT", [128, MAXF, 1024], BF16)
        asb = [sb(s45, "asb%d" % i, [128, 1026], F32) for i in range(2)]
        ct = [sb(s45, "ct%d" % i, [128, 1024], F32) for i in range(2)]
        dosb = [sb(s45, "dosb%d" % i, [128, 1024], F32) for i in range(2)]
        pin = [sb(s45, "pin%d" % i, [128, 1024], F32) for i in range(2)]
        dxp = [sb(s45, "dxp%d" % i, [128, 8, 128], F32) for i in range(2)]
        dyo = [sb(s45, "dyo%d" % i, [128, 8, 128], F32) for i in range(2)]
        wdv = w_down.rearrange("(fb p) d -> p fb d", p=128)
        up_blocks = {}
        d_blocks = {}
        ffn_ada = []
        for pp, (fb0, nfb) in enumerate(PARTS):
            up_blocks[pp] = []
            for f in range(nfb):
                up_blocks[pp].append((wblock(w_up, (fb0 + f) * 128), wblock(w_up, DFF + (fb0 + f) * 128)))
                if pp == 0:
                    for _ in range(2 if f < 10 else 1):
                        ffn_ada.append((f, len(ffn_ada), wblock(w_ada, (160 + len(ffn_ada)) * 128)))
            d_blocks[pp] = [ws.add(wdv[:, fb0:fb0 + nfb, db * 128:(db + 1) * 128], nfb) for db in range(32)]
        hctr = 0
        for pp, (fb0, nfb) in enumerate(PARTS):
            last_part = (pp == NPARTS - 1)
            for f in range(nfb):
                fb = fb0 + f
                wa, wak = ws.get(up_blocks[pp][f][0], ahead=2)
                wg, wgk = ws.get(up_blocks[pp][f][1], ahead=2)
                a_s = asb[f % 2]
                ak = "asb%d" % (f % 2)
                c_t = ct[f % 2]
                ck = "ct%d" % (f % 2)
                pg = pS[f % 2]
                pgk = "pS%d" % (f % 2)
                hcol = 2 * (hctr % 128)
                hctr += 1

                def mh():
                    last = None
                    for k in range(NKC):
                        last = E["pe"].matmul(pb[2][:, hcol:hcol + 2], lhsT=wa[:, k, :], rhs=h2T[:, k, 0:1026:1025],
                                              start=(k == 0), stop=(k == NKC - 1))
                    return last
                T.op("pe", mh, reads=[wak, "h2T"], writes=["pb2"])
                T.op("act", lambda: E["act"].copy(out=a_s[:, 0:1026:1025], in_=pb[2][:, hcol:hcol + 2]),
                     reads=["pb2"], writes=[ak])
                for hf in range(2):
                    pa = pb[hf]
                    pak = "pb%d" % hf

                    def ma():
                        last = None
                        for k in range(NKC):
                            last = E["pe"].matmul(pa[:, 0:512], lhsT=wa[:, k, :], rhs=h2T[:, k, 1 + hf * 512:1 + (hf + 1) * 512],
                                                  start=(k == 0), stop=(k == NKC - 1))
                        return last
                    T.op("pe", ma, reads=[wak, "h2T"], writes=[pak])
                    T.op("act", lambda: E["act"].copy(out=a_s[:, 1 + hf * 512:1 + (hf + 1) * 512], in_=pa[:, 0:512]),
                         reads=[pak], writes=[ak])

                    def mg():
                        last = None
                        for k in range(NKC):
                            last = E["pe"].matmul(pg[:, hf * 512:(hf + 1) * 512], lhsT=wg[:, k, :],
                                                  rhs=h2T[:, k, 1 + hf * 512:1 + (hf + 1) * 512],
                                                  start=(k == 0), stop=(k == NKC - 1))
                        return last
                    T.op("pe", mg, reads=[wgk, "h2T"], writes=[pgk])
                T.op("dve", lambda: E["dve"].tensor_scalar(out=c_t[:], in0=a_s[:, 1:1025], scalar1=cw[:, 1, fb:fb + 1],
                                                          scalar2=cb_[:, fb:fb + 1], op0=ALU.mult, op1=ALU.add),
                     reads=[ak, "cw", "cb"], writes=[ck])
                T.op("dve", lambda: E["dve"].scalar_tensor_tensor(out=c_t[:], in0=a_s[:, 0:1024], scalar=cw[:, 0, fb:fb + 1],
                                                                  in1=c_t[:], op0=ALU.mult, op1=ALU.add),
                     reads=[ak, ck], writes=[ck])
                T.op("dve", lambda: E["dve"].scalar_tensor_tensor(out=c_t[:], in0=a_s[:, 2:1026], scalar=cw[:, 2, fb:fb + 1],
                                                                  in1=c_t[:], op0=ALU.mult, op1=ALU.add),
                     reads=[ak, ck], writes=[ck])
                T.op("act", lambda: E["act"].activation(out=c_t[:], in_=c_t[:], func=AF.Silu), reads=[ck], writes=[ck])
                T.op("dve", lambda: E["dve"].tensor_tensor(out=zT[:, f, :], in0=c_t[:], in1=pg[:, 0:1024], op=ALU.mult),
                     reads=[ck, pgk], writes=["zT%d" % f])
                if pp == 0:
                    for (f_, j_, bi_) in ffn_ada:
                        if f_ == f:
                            ada_group([bi_], 160 + j_, pb[3][:, 2 * j_:2 * j_ + 2], "pb3", evac=False)
            if pp == 0:
                psv3 = pb[3][:, 0:64].rearrange("p (b j) -> p b j", j=2)
                T.op("dve", lambda: E["dve"].tensor_tensor(out=vec[:, V_GT2, :], in0=psv3[:, :, 0], in1=bada[:, 160:192], op=ALU.add),
                     reads=["pb3", "bada"], writes=["vec5"])
                dump("zT0", zT[:, 0, :], [128, 1024], BF16)
            zkeys = ["zT%d" % f for f in range(nfb)]
            def d_mm(db):
                wd, wdk = ws.get(d_blocks[pp][db])
                ps = pS[db % 2]
                psk = "pS%d" % (db % 2)
                if pp > 0:
                    T.dma("sp", "pin%d" % (db % 2), out=pin[db % 2][:], in_=part_d[db], reads=["part_d%d" % db], writes=["pin%d" % (db % 2)])
                if last_part:
                    T.dma("sp", "dxp%d" % (db % 2), out=dxp[db % 2][:], in_=xn_own[:, :, db * 128:(db + 1) * 128], writes=["dxp%d" % (db % 2)])

                def md():
                    last = None
                    for hf in range(2):
                        for f in range(nfb):
                            last = E["pe"].matmul(ps[:, hf * 512:(hf + 1) * 512], lhsT=wd[:, f, :], rhs=zT[:, f, hf * 512:(hf + 1) * 512],
                                                  start=(f == 0), stop=(f == nfb - 1))
                    return last
                T.op("pe", md, reads=[wdk] + zkeys, writes=[psk])

            def d_rest(db):
                ps = pS[db % 2]
                psk = "pS%d" % (db % 2)
                o = dosb[db % 2]
                ok = "dosb%d" % (db % 2)
                p_i = pin[db % 2]
                pik = "pin%d" % (db % 2)
                x_p = dxp[db % 2]
                xpk = "dxp%d" % (db % 2)
                y_o = dyo[db % 2]
                yok = "dyo%d" % (db % 2)
                if pp == 0:
                    T.op("act", lambda: E["act"].copy(out=o[:], in_=ps[:, 0:1024]), reads=[psk], writes=[ok])
                else:
                    T.op("dve", lambda: E["dve"].tensor_tensor(out=o[:], in0=ps[:, 0:1024], in1=p_i[:], op=ALU.add),
                         reads=[psk, pik], writes=[ok])
                if not last_part:
                    T.dma("sp", ok, out=part_d[db], in_=o[:], reads=[ok], writes=["part_d%d" % db])
                else:
                    T.op("act", lambda: E["act"].activation(out=o[:], in_=o[:], func=AF.Identity, scale=vec[:, V_GT2, db:db + 1]),
                         reads=[ok, "vec5"], writes=[ok])
                    for g4 in range(2):
                        bank = pb[g4]
                        bkey = "pb%d" % g4

                        def tr():
                            last = None
                            for j in range(4):
                                tj = 4 * g4 + j
                                last = E["pe"].transpose(bank[:, j * 128:(j + 1) * 128], o[:, tj * 128:(tj + 1) * 128], ident[:])
                            return last
                        T.op("pe", tr, reads=[ok, "ident"], writes=[bkey])
                        T.op("dve", lambda: E["dve"].tensor_tensor(out=y_o[:, 4 * g4:4 * g4 + 4, :],
                                                                  in0=bank[:, 0:512].rearrange("p (s d) -> p s d", d=128),
                                                                  in1=x_p[:, 4 * g4:4 * g4 + 4, :], op=ALU.add),
                             reads=[bkey, xpk], writes=[yok])
                    T.dma("sp", yok, out=y_own[:, :, db * 128:(db + 1) * 128], in_=y_o[:], reads=[yok], writes=["y_d"])

            d_mm(0)
            for db in range(32):
                if db + 1 < 32:
                    d_mm(db + 1)
                d_rest(db)
        T.barrier()
    if stop_after == "S5":
        if "y" in dbg.get("dump", ()):
            dbg_outs.append("y_d")
        return finish(nc, es, T, dbg_outs)

    with ExitStack() as s6:
        gfull = sb(s6, "gfull", [128, D], F32)
        yt = [sb(s6, "yt%d" % i, [128, D], F32) for i in range(4)]
        junk = sb(s6, "fjunk", [128, D], BF16)
        st = sb(s6, "fst", [128, 32], F32)
        T.dma("sp", "gfull", out=gfull[:], in_=gf.partition_broadcast(128), writes=["gfull"])
        T.op("dve", lambda: E["dve"].memset(st[:], 0.0), writes=["fst"])
        for j in range(8):
            y_t = yt[j % 4]
            yk = "yt%d" % (j % 4)
            T.dma("sp", yk, out=y_t[:], in_=y_d[j * 128:(j + 1) * 128, :], writes=[yk])
            ss = st[:, 4 * j:4 * j + 1]
            ms = st[:, 4 * j + 1:4 * j + 2]
            sd = st[:, 4 * j + 2:4 * j + 3]
            rs = st[:, 4 * j + 3:4 * j + 4]
            sk = "fs%d" % j
            T.op("act", lambda: E["act"].activation(out=junk[:], in_=y_t[:], func=AF.Square, accum_out=ss),
                 reads=[yk, "fst"], writes=["fjunk", sk])
            T.op("dve", lambda: E["dve"].tensor_scalar(out=ms, in0=ss, scalar1=1.0 / D, scalar2=EPS, op0=ALU.mult, op1=ALU.add),
                 reads=[sk], writes=[sk + "m"])
            T.op("act", lambda: E["act"].activation(out=sd, in_=ms, func=AF.Sqrt), reads=[sk + "m"], writes=[sk + "s"])
            T.op("dve", lambda: E["dve"].reciprocal(out=rs, in_=sd), reads=[sk + "s"], writes=[sk + "r"])
            T.op("dve", lambda: E["dve"].scalar_tensor_tensor(out=y_t[:], in0=y_t[:], scalar=rs, in1=gfull[:], op0=ALU.mult, op1=ALU.mult),
                 reads=[sk + "r", yk, "gfull"], writes=[yk])
            T.dma("sp", yk, out=out[j * 128:(j + 1) * 128, :], in_=y_t[:], reads=[yk], writes=["out"])
        T.barrier()
    return finish(nc, es, T, dbg_outs)


def finish(nc, es, T, dbg_outs):
    T.barrier()
    nc._dbg_outs = dbg_outs
    return nc


def _tables(rpb):
    k = np.arange(128)
    kr, kc = k // 64, k % 64
    q = np.arange(128)
    qr, qc = q // 64, q % 64
    c0 = np.clip(qc - 8, 0, 48)
    colok = (kc[:, None] >= c0[None, :]) & (kc[:, None] < c0[None, :] + 16)
    dc = np.clip(kc[:, None] - qc[None, :], -15, 15) + 15
    tabs = np.empty((16, 128, 7, 128), np.float32)
    for dti in range(7):
        dr = 2 * (dti - 3) + kr[:, None] - qr[None, :] + 7
        g = rpb[:, dr, dc]
        tabs[:, :, dti, :] = np.where(colok[None], g, np.float32(NEG))
    return tabs.reshape(16, 128, 7 * 128)


def _rmask(core):
    m = np.zeros((2, NPAIR, 2, 64), np.float32)
    for pi, (jl, t) in enumerate(PAIRS):
        for kr in range(2):
            for qr in range(2):
                qrow = 2 * (8 * core + jl) + qr
                krow = 2 * (8 * core + t) + kr
                r0 = min(max(qrow - 4, 0), 120) if 0 <= qrow <= 127 else qrow - 4
                ok = (r0 <= krow <= r0 + 7) and (0 <= krow <= 127)
                m[kr, pi, qr, :] = 0.0 if ok else NEG
    return m.reshape(2, NPAIR * 128)


def make_in_maps(inputs, cores):
    f = lambda a: np.ascontiguousarray(np.asarray(a, dtype=np.float32))
    x = f(inputs["x"])[0]
    shared = {
        "ctx": f(inputs["ctx"])[0],
        "c2": np.ascontiguousarray(np.stack([f(inputs["c"])[0], f(inputs["c_ctx"])])),
        "w_ada": f(inputs["w_ada"])[0], "b_ada": f(inputs["b_ada"])[0], "g_norm1": f(inputs["g_norm1"])[0],
        "w_in": f(inputs["w_in"])[0], "a_ln_g": f(inputs["a_ln_g"])[0], "a_ln_b": f(inputs["a_ln_b"])[0],
        "wsT": np.ascontiguousarray(f(inputs["a_w_s"])[0].transpose(2, 0, 1)),
        "a_b_s": f(inputs["a_b_s"])[0].reshape(-1),
        "tab": _tables(f(inputs["na_rpb"])[0]),
        "rowsel": np.ascontiguousarray(np.stack([(np.arange(128) < 64), (np.arange(128) >= 64)]).astype(np.float32)),
        "ident": np.eye(128, dtype=np.float32),
        "w_out": f(inputs["w_out"])[0], "g_norm2": f(inputs["g_norm2"])[0], "w_up": f(inputs["w_up"])[0],
        "conv_w": f(inputs["conv_w"])[0], "conv_b": f(inputs["conv_b"])[0], "w_down": f(inputs["w_down"])[0],
        "g_final": f(inputs["g_final"]),
    }
    maps = []
    order = list(range(-1, 9)) + [-3, -2, 9]
    for core in cores:
        xa = np.zeros((1664, D), np.float32)
        for s, t in enumerate(order):
            g = 8 * core + t
            if 0 <= g < 64:
                xa[s * 128:(s + 1) * 128] = x[g * 128:(g + 1) * 128]
        flags = np.zeros((128, 2), np.float32)
        flags[:, 0] = 1.0 if core > 0 else 0.0
        flags[:, 1] = 1.0 if core < NCORES - 1 else 0.0
        m = dict(shared)
        m["xa"] = xa
        m["rmask"] = _rmask(core)
        m["flags"] = flags
        maps.append(m)
    return maps


def kernel(**inputs):
    nc = build_nc()
    maps = make_in_maps(inputs, list(range(NCORES)))
    maps = [{k: m[k] for k in nc._in_names} for m in maps]
    res = run_bass_kernel_spmd(nc, maps, core_ids=list(range(NCORES)))
    outs = [np.asarray(r["out"], dtype=np.float32) for r in res.results]
    return np.concatenate(outs, axis=0).reshape(1, NCORES * 1024, D)
```

```python
from contextlib import ExitStack
import numpy as np
import concourse.bass as bass
import concourse.mybir as mybir
from concourse.bass_utils import run_bass_kernel_spmd

F32 = mybir.dt.float32
BF16 = mybir.dt.bfloat16
AF = mybir.ActivationFunctionType
ALU = mybir.AluOpType

D = 4096
NKC = 32
DFF = 11008
NFB = 86
NCORES = 8
EPS = 1e-6
NEG = -30000.0
NPARTS = 4
PARTS = [(0, 22), (22, 22), (44, 21), (65, 21)]

KEYS = {-1: [-3, -2, -1, 0, 1], 0: [-2, -1, 0, 1, 2, 3], 1: [-1, 0, 1, 2, 3], 2: [0, 1, 2, 3, 4],
        3: [1, 2, 3, 4, 5], 4: [2, 3, 4, 5, 6], 5: [3, 4, 5, 6, 7], 6: [4, 5, 6, 7, 8],
        7: [4, 5, 6, 7, 8, 9], 8: [6, 7, 8, 9]}
QTILES = list(range(-1, 9))
PAIRS = [(jl, t) for jl in QTILES for t in KEYS[jl]]
PAIR_IDX = {p: i for i, p in enumerate(PAIRS)}
NPAIR = len(PAIRS)


def slot_of(t):
    if -1 <= t <= 8:
        return t + 1
    return {-3: 10, -2: 11, 9: 12}[t]


class Trk:
    def __init__(self, nc, es):
        self.nc = nc
        self.es = es
        self.E = {"pe": nc.tensor, "act": nc.scalar, "dve": nc.vector, "pool": nc.gpsimd, "sp": nc.sync}
        self.sem = {e: es.enter_context(nc.semaphore("c_" + e)) for e in self.E}
        self.cnt = {e: 0 for e in self.E}
        self.seen = {e: {} for e in self.E}
        self.res = {}
        self.dsem = {}
        self.pending = {}

    def _wait(self, eng, tok):
        key, sem, val = tok
        if self.seen[eng].get(key, 0) >= val:
            return
        self.E[eng].wait_ge(sem, val)
        self.seen[eng][key] = val

    def _deps(self, eng, reads, writes):
        toks = []
        for r in reads:
            st = self.res.get(r)
            if st and st[0]:
                toks.append(st[0])
        for w in writes:
            st = self.res.get(w)
            if st:
                if st[0]:
                    toks.append(st[0])
                toks.extend(st[1].values())
        for t in toks:
            if eng == "pe" and t[0] == "pe":
                continue
            self._wait(eng, t)

    def _commit(self, tok, reads, writes):
        for r in reads:
            st = self.res.setdefault(r, [None, {}])
            st[1][tok[0]] = tok
        for w in writes:
            self.res[w] = [tok, {}]

    def op(self, eng, fn, reads=(), writes=()):
        self._deps(eng, reads, writes)
        ins = fn()
        self.cnt[eng] += 1
        ins.then_inc(self.sem[eng], 1)
        self._commit((eng, self.sem[eng], self.cnt[eng]), reads, writes)

    def dma(self, eng, slot, out, in_, reads=(), writes=()):
        self._deps(eng, reads, writes)
        if slot not in self.dsem:
            self.dsem[slot] = [self.es.enter_context(self.nc.semaphore("d_" + slot)), 0]
        s = self.dsem[slot]
        ins = self.E[eng].dma_start(out=out, in_=in_)
        s[1] += 16
        ins.then_inc(s[0], 16)
        tok = ("d_" + slot, s[0], s[1])
        self._commit(tok, reads, writes)
        self.pending[tok[0]] = tok

    def barrier(self):
        for e in self.E:
            if e != "sp" and self.cnt[e] > 0:
                self._wait("sp", (e, self.sem[e], self.cnt[e]))
        for tok in self.pending.values():
            self._wait("sp", tok)
        self.pending = {}
        ins = self.E["sp"].sem_inc(self.sem["sp"], 1)
        self.cnt["sp"] += 1
        tok = ("sp", self.sem["sp"], self.cnt["sp"])
        for e in self.E:
            if e != "sp":
                self._wait(e, tok)
        self.res = {}


def build_nc(dbg=None):
    nc = bass.Bass("TRN2", target_bir_lowering=False)
    es = ExitStack()
    _build(nc, es, dbg or {})
    es.close()
    return nc


def _build(nc, es, dbg):
    stop_after = dbg.get("stop_after")

    in_names = []

    def inp(name, shape, dt=F32):
        in_names.append(name)
        return nc.dram_tensor(name, list(shape), dt, kind="ExternalInput").ap()
    nc._in_names = in_names

    xa = inp("xa", [1664, D])
    ctx_in = inp("ctx", [256, D])
    c2 = inp("c2", [2, D])
    w_ada = inp("w_ada", [D, 6 * D])
    b_ada = inp("b_ada", [6 * D])
    g1 = inp("g_norm1", [D])
    ident_in = inp("ident", [128, 128])
    flags_in = inp("flags", [128, 2])
    g2 = inp("g_norm2", [D])
    out = nc.dram_tensor("out", [1024, D], F32, kind="ExternalOutput").ap()

    ycat_d = nc.dram_tensor("ycat_d", [32, 128, 1026], BF16).ap()
    xnew_d = nc.dram_tensor("xnew_d", [1026, D], F32).ap()
    part_d = nc.dram_tensor("part_d", [32, 128, 1024], F32).ap()
    y_d = nc.dram_tensor("y_d", [1024, D], F32).ap()

    T = Trk(nc, es)
    E = T.E
    dbg_outs = []

    def sb(st, name, shape, dt):
        return st.enter_context(nc.sbuf_tensor("s_" + name, list(shape), dt))

    def dump(name, ap, shape, dt):
        if name not in dbg.get("dump", ()):
            return
        d = nc.dram_tensor("dbg_" + name, list(shape), dt, kind="ExternalOutput").ap()
        T.barrier()
        T.dma("sp", "dbg_" + name, out=d, in_=ap)
        T.barrier()
        dbg_outs.append("dbg_" + name)

    pb = [es.enter_context(nc.psum_tensor("pb%d" % i, [128, 512], F32)) for i in range(4)]
    pS = [es.enter_context(nc.psum_tensor("pS%d" % i, [128, 1024], F32)) for i in range(2)]

    ident = sb(es, "ident", [128, 128], F32)
    vec = sb(es, "vec", [128, 12, 32], F32)
    gfb = sb(es, "gfb", [128, 32], F32)
    flags = sb(es, "flags", [128, 2], F32)
    bada = sb(es, "bada", [128, 192], F32)
    scT = sb(es, "scT", [128, 32, 2], BF16)
    NWB = 4
    wbuf = [sb(es, "wbuf%d" % i, [128, 32, 128], BF16) for i in range(NWB)]
    V_SH1, V_SC1, V_GT1, V_SH2, V_SC2, V_GT2, V_CSH1, V_CSC1, V_A1, V_CA1, V_A2, V_G = range(12)

    T.dma("sp", "ident", out=ident[:], in_=ident_in[:, :], writes=["ident"])
    T.dma("sp", "flags", out=flags[:], in_=flags_in[:, :], writes=["flags"])

    class WS:
        def __init__(self):
            self.blocks = []
            self.issued = 0

        def add(self, ap, nk):
            self.blocks.append((ap, nk))
            return len(self.blocks) - 1

        def get(self, i, ahead=NWB - 1):
            while self.issued < len(self.blocks) and self.issued <= i + ahead:
                j = self.issued
                ap, nk = self.blocks[j]
                b = j % NWB
                T.dma("pool", "wbuf%d" % b, out=wbuf[b][:, 0:nk, :], in_=ap, writes=["wbuf%d" % b])
                self.issued += 1
            return wbuf[i % NWB], "wbuf%d" % (i % NWB)

    ws = WS()

    def wblock(w_ap, c0, nk=NKC, k0=0):
        v = w_ap.rearrange("(kc p) c -> p kc c", p=128)
        return ws.add(v[:, k0:k0 + nk, c0:c0 + 128], nk)

    pbrot = [0]

    def linear(bi, chunks, evac, banks=(0, 1, 2)):
        wb, wkey = ws.get(bi)
        nk = ws.blocks[bi][1]
        for ci, (src, skey, c0, n) in enumerate(chunks):
            b = banks[pbrot[0] % len(banks)]
            pbrot[0] += 1
            bank = pb[b]
            bkey = "pb%d" % b

            def mm():
                last = None
                for k in range(nk):
                    last = E["pe"].matmul(bank[:, 0:n], lhsT=wb[:, k, :], rhs=src[:, k, c0:c0 + n],
                                          start=(k == 0), stop=(k == nk - 1))
                return last
            T.op("pe", mm, reads=[wkey, skey], writes=[bkey])
            evac(ci, bank, bkey, n)

    with ExitStack() as s0:
        scf = sb(s0, "scf", [128, 32, 2], F32)
        with nc.allow_non_contiguous_dma(reason="small vector layouts"):
            for j in range(2):
                T.dma("sp", "scf", out=scf[:, :, j], in_=c2[j].rearrange("(kc p) -> p kc", p=128), writes=["scf"])
            T.dma("sp", "bada", out=bada[:], in_=b_ada.rearrange("(b p) -> p b", p=128), writes=["bada"])
            T.dma("sp", "vecg", out=vec[:, V_G, :], in_=g1.rearrange("(b p) -> p b", p=128), writes=["vecg"])
            T.dma("sp", "gfb", out=gfb[:], in_=g2.rearrange("(b p) -> p b", p=128), writes=["gfb"])
        T.op("act", lambda: E["act"].activation(out=scT[:], in_=scf[:], func=AF.Silu), reads=["scf"], writes=["scT"])
        T.barrier()

    def ada_group(blocks, cb0, ps, pkey, ctx_too=False, evac=True):
        n = len(blocks)
        for j, bi in enumerate(blocks):
            wb, wkey = ws.get(bi)

            def mm():
                last = None
                for k in range(NKC):
                    last = E["pe"].matmul(ps[:, 2 * j:2 * j + 2], lhsT=wb[:, k, :], rhs=scT[:, k, :],
                                          start=(k == 0), stop=(k == NKC - 1))
                return last
            T.op("pe", mm, reads=[wkey, "scT"], writes=[pkey])
        if dbg.get("no_evac") or not evac:
            return
        v = cb0 // 32
        blk0 = cb0 % 32
        psv = ps[:, 0:2 * n].rearrange("p (b j) -> p b j", j=2)
        T.op("dve", lambda: E["dve"].tensor_tensor(out=vec[:, v, blk0:blk0 + n], in0=psv[:, :, 0],
                                                  in1=bada[:, cb0:cb0 + n], op=ALU.add),
             reads=[pkey, "bada"], writes=["vec%d_%d" % (v, blk0)])
        if ctx_too:
            T.op("dve", lambda: E["dve"].tensor_tensor(out=vec[:, V_CSH1 + v, blk0:blk0 + n], in0=psv[:, :, 1],
                                                      in1=bada[:, cb0:cb0 + n], op=ALU.add),
                 reads=[pkey, "bada"], writes=["vecc%d_%d" % (v, blk0)])

    w_in = inp("w_in", [D, 10240])
    ln_g = inp("a_ln_g", [2048])
    ln_b = inp("a_ln_b", [2048])
    wsT_in = inp("wsT", [128, 16, 128])
    b_s = inp("a_b_s", [2048])
    tab = inp("tab", [16, 128, 7 * 128])
    rmask_in = inp("rmask", [2, NPAIR * 128])
    rowsel_in = inp("rowsel", [2, 128])

    def norm_tiles(st, tiles, tagp, raw=False, between=None, nbuf=2):
        xt = [sb(st, tagp + "xt%d" % i, [128, D], F32) for i in range(nbuf)]
        junk = sb(st, tagp + "junk", [128, D], BF16)
        stat = sb(st, tagp + "stat", [128, 4 * len(tiles)], F32)
        T.op("dve", lambda: E["dve"].memset(stat[:], 0.0), writes=[tagp + "stat"])
        for i, (src, n, dst_fn, dkey, ai, bi) in enumerate(tiles):
            x_t = xt[i % nbuf]
            xk = tagp + "xt%d" % (i % nbuf)
            dkey0 = dkey
            T.dma("sp", xk, out=x_t[0:n, :], in_=src, writes=[xk])
            ss = stat[0:n, 4 * i:4 * i + 1]
            ms = stat[0:n, 4 * i + 1:4 * i + 2]
            sd = stat[0:n, 4 * i + 2:4 * i + 3]
            rs = stat[0:n, 4 * i + 3:4 * i + 4]
            sk = tagp + "st%d" % i
            T.op("act", lambda: E["act"].activation(out=junk[0:n, :], in_=x_t[0:n, :], func=AF.Square, accum_out=ss),
                 reads=[xk, tagp + "stat"], writes=[tagp + "junk", sk])
            T.op("dve", lambda: E["dve"].tensor_scalar(out=ms, in0=ss, scalar1=1.0 / D, scalar2=EPS,
                                                      op0=ALU.mult, op1=ALU.add), reads=[sk], writes=[sk + "m"])
            T.op("act", lambda: E["act"].activation(out=sd, in_=ms, func=AF.Sqrt), reads=[sk + "m"], writes=[sk + "s"])
            T.op("dve", lambda: E["dve"].reciprocal(out=rs, in_=sd), reads=[sk + "s"], writes=[sk + "r"])
            T.op("dve", lambda: E["dve"].tensor_scalar(out=x_t[0:n, :], in0=x_t[0:n, :], scalar1=rs, scalar2=None,
                                                      op0=ALU.mult), reads=[sk + "r", xk], writes=[xk])
            for gq in range(8):
                b = gq % 4
                bank = pb[b]
                bkey = "pb%d" % b

                def tr():
                    last = None
                    for j in range(4):
                        blk = gq * 4 + j
                        last = E["pe"].transpose(bank[:, j * n:(j + 1) * n], x_t[0:n, blk * 128:(blk + 1) * 128],
                                                 ident[0:n, 0:n])
                    return last
                T.op("pe", tr, reads=[xk, "ident"], writes=[bkey])
                for j in range(4):
                    blk = gq * 4 + j
                    dst = dst_fn(blk)
                    dkey = dkey0 + "_%d_%d" % (i, blk)
                    if raw:
                        if gq % 2 == 0:
                            T.op("act", lambda: E["act"].copy(out=dst, in_=bank[:, j * n:(j + 1) * n]), reads=[bkey], writes=[dkey])
                        else:
                            T.op("dve", lambda: E["dve"].tensor_copy(out=dst, in_=bank[:, j * n:(j + 1) * n]), reads=[bkey], writes=[dkey])
                    elif gq % 2 == 0:
                        T.op("act", lambda: E["act"].activation(out=dst, in_=bank[:, j * n:(j + 1) * n], func=AF.Identity,
                                                                bias=vec[:, bi, blk:blk + 1], scale=vec[:, ai, blk:blk + 1]),
                             reads=[bkey], writes=[dkey])
                    else:
                        T.op("dve", lambda: E["dve"].tensor_scalar(out=dst, in0=bank[:, j * n:(j + 1) * n],
                                                                  scalar1=vec[:, ai, blk:blk + 1],
                                                                  scalar2=vec[:, bi, blk:blk + 1],
                                                                  op0=ALU.mult, op1=ALU.add),
                             reads=[bkey], writes=[dkey])
            if between is not None:
                between(i)

    with ExitStack() as s12:
        hTm = sb(s12, "hTm", [128, NKC, 1280], BF16)
        with ExitStack() as sat:
            hTh = sb(sat, "hTh", [128, NKC, 384], BF16)
            hTc = sb(sat, "hTc", [128, NKC, 256], BF16)
            with ExitStack() as s1:
                tiles = []
                for slot in range(13):
                    if slot < 10:
                        f = (lambda blk, slot=slot: hTm[:, blk, slot * 128:(slot + 1) * 128])
                        key = "hTm"
                    else:
                        f = (lambda blk, slot=slot: hTh[:, blk, (slot - 10) * 128:(slot - 9) * 128])
                        key = "hTh"
                    tiles.append((xa[slot * 128:(slot + 1) * 128, :], 128, f, key, V_A1, V_SH1))
                for ct in range(2):
                    f = (lambda blk, ct=ct: hTc[:, blk, ct * 128:(ct + 1) * 128])
                    tiles.append((ctx_in[ct * 128:(ct + 1) * 128, :], 128, f, "hTc", V_CA1, V_CSH1))
                ada0 = [wblock(w_ada, cb * 128) for cb in range(64)]

                def between(i):
                    lo, hi = (i * 16) // 15, ((i + 1) * 16) // 15
                    for g_ in range(lo, hi):
                        if dbg.get("bar_int"):
                            T.barrier()
                        ada_group(ada0[4 * g_:4 * g_ + 4], 4 * g_, pS[g_ % 2][:, 8 * g_:8 * g_ + 8], "pSa_%d" % g_, ctx_too=True)
                if dbg.get("no_interleave"):
                    norm_tiles(s1, tiles, "n1", raw=True)
                    for i_ in range(15):
                        between(i_)
                else:
                    norm_tiles(s1, tiles, "n1", raw=True, between=between)
                T.barrier()
                T.op("dve", lambda: E["dve"].scalar_tensor_tensor(out=vec[:, V_A1, :], in0=vec[:, V_SC1, :], scalar=1.0,
                                                                  in1=vec[:, V_G, :], op0=ALU.add, op1=ALU.mult), writes=["vecA1"])
                T.op("dve", lambda: E["dve"].scalar_tensor_tensor(out=vec[:, V_CA1, :], in0=vec[:, V_CSC1, :], scalar=1.0,
                                                                  in1=vec[:, V_G, :], op0=ALU.add, op1=ALU.mult), writes=["vecA1"])
                k_ = 0
                for (ht, hk, ai, bi) in ((hTm, "hTm", V_A1, V_SH1), (hTh, "hTh", V_A1, V_SH1), (hTc, "hTc", V_CA1, V_CSH1)):
                    for blk in range(NKC):
                        if k_ % 2 == 0:
                            T.op("act", lambda: E["act"].activation(out=ht[:, blk, :], in_=ht[:, blk, :], func=AF.Identity,
                                                                    bias=vec[:, bi, blk:blk + 1], scale=vec[:, ai, blk:blk + 1]),
                                 reads=["vecA1"], writes=[hk + "%d" % blk])
                        else:
                            T.op("dve", lambda: E["dve"].tensor_scalar(out=ht[:, blk, :], in0=ht[:, blk, :],
                                                                      scalar1=vec[:, ai, blk:blk + 1], scalar2=vec[:, bi, blk:blk + 1],
                                                                      op0=ALU.mult, op1=ALU.add),
                                 reads=["vecA1"], writes=[hk + "%d" % blk])
                        k_ += 1
                T.barrier()
            dump("hTm", hTm[:], [128, NKC, 1280], BF16)
            if stop_after == "S1":
                return finish(nc, es, T, dbg_outs)

            QTs = [sb(sat, "QT%d" % i, [128, 1280], BF16) for i in range(2)]
            KTs = [sb(sat, "KT%d" % i, [128, 1920], BF16) for i in range(2)]
            Vt = sb(sat, "Vt", [128, 15, 130], BF16)
            Th = [sb(sat, "Th%d" % i, [128, 7 * 128], F32) for i in range(2)]
            ssb = [sb(sat, "ssb%d" % i, [128, 768], F32) for i in range(2)]
            Pt = [sb(sat, "Pt%d" % i, [128, 1024], BF16) for i in range(2)]
            ybtm = [sb(sat, "ybtm%d" % i, [128, 128], F32) for i in range(2)]
            rden = sb(sat, "rden", [128, 16], F32)
            ybT = [sb(sat, "ybT%d" % i, [128, 1280], BF16) for i in range(1)]
            rmask = sb(sat, "rmask", [2, NPAIR * 128], BF16)
            rowsel = sb(sat, "rowsel", [2, 128], BF16)
            T.dma("pool", "rmask", out=rmask[:], in_=rmask_in[:, :], writes=["rmask"])
            T.dma("pool", "rowsel", out=rowsel[:], in_=rowsel_in[:, :], writes=["rowsel"])
            T.op("dve", lambda: E["dve"].memset(Vt[:], 1.0), writes=["Vt"])

            main_chunks = [(hTm, "hTm", 0, 512), (hTm, "hTm", 512, 512), (hTm, "hTm", 1024, 256)]
            kv_chunks = main_chunks + [(hTh, "hTh", 0, 384), (hTc, "hTc", 0, 256)]
            kv_off = [0, 512, 1024, 1280, 1664]
            q_chunks = [(hTm, "hTm", 127, 512), (hTm, "hTm", 639, 512), (hTm, "hTm", 1151, 2)]
            qscale = 128.0 ** -0.5
            blk_q = [None] * 16
            blk_k = [None] * 16
            blk_v = [None] * 16
            att_ada = [None] * 16
            blk_q[0] = wblock(w_in, 4096)
            blk_k[0] = wblock(w_in, 6144)
            for h in range(16):
                blk_v[h] = wblock(w_in, 8192 + 128 * h)
                ada_h = [None] * 6
                if h + 1 < 16:
                    blk_q[h + 1] = wblock(w_in, 4096 + 128 * (h + 1))
                ada_h[0] = wblock(w_ada, (64 + 6 * h + 0) * 128)
                ada_h[1] = wblock(w_ada, (64 + 6 * h + 1) * 128)
                if h + 1 < 16:
                    blk_k[h + 1] = wblock(w_in, 6144 + 128 * (h + 1))
                for j in range(2, 6):
                    ada_h[j] = wblock(w_ada, (64 + 6 * h + j) * 128)
                att_ada[h] = ada_h

            ipb = [0]

            def proj_item(hn, kind):
                QTn, KTn = QTs[hn % 2], KTs[hn % 2]
                qk_, kk_ = "QT%d" % (hn % 2), "KT%d" % (hn % 2)

                def item():
                    wb, wkey = ws.get(blk_q[hn] if kind == "q" else blk_k[hn])
                    chunks = q_chunks if kind == "q" else kv_chunks
                    for ci, (src, skey, c0, n) in enumerate(chunks):
                        b = ipb[0] % 2
                        ipb[0] += 1
                        bank = pb[b]
                        bkey = "pb%d" % b

                        def mm():
                            last = None
                            for k in range(NKC):
                                last = E["pe"].matmul(bank[:, 0:n], lhsT=wb[:, k, :], rhs=src[:, k, c0:c0 + n],
                                                      start=(k == 0), stop=(k == NKC - 1))
                            return last
                        T.op("pe", mm, reads=[wkey, skey], writes=[bkey])
                        if kind == "q":
                            T.op("act", lambda: E["act"].activation(out=QTn[:, c0:c0 + n], in_=bank[:, 0:n], func=AF.Identity, scale=qscale),
                                 reads=[bkey], writes=[qk_])
                        else:
                            o0 = kv_off[ci]
                            T.op("dve", lambda: E["dve"].tensor_copy(out=KTn[:, o0:o0 + n], in_=bank[:, 0:n]), reads=[bkey], writes=[kk_])
                return item

            def ada_item(h_, j_):
                def item():
                    ada_group([att_ada[h_][j_]], 64 + 6 * h_ + j_, pb[2][:, 12 * h_ + 2 * j_:12 * h_ + 2 * j_ + 2], "pb2", evac=False)
                return item

            def head_items(h_):
                items = []
                if h_ + 1 < 16:
                    items.append(proj_item(h_ + 1, "q"))
                items.append(ada_item(h_, 0))
                items.append(ada_item(h_, 1))
                if h_ + 1 < 16:
                    items.append(proj_item(h_ + 1, "k"))
                for j_ in range(2, 6):
                    items.append(ada_item(h_, j_))
                return items

            proj_item(0, "q")()
            proj_item(0, "k")()


            def v_src(s_):
                if s_ < 10:
                    return hTm, "hTm", s_ * 128
                if s_ < 13:
                    return hTh, "hTh", (s_ - 10) * 128
                return hTc, "hTc", (s_ - 13) * 128

            for h in range(16):
                QT, KT = QTs[h % 2], KTs[h % 2]
                QTk, KTk = "QT%d" % (h % 2), "KT%d" % (h % 2)
                th = Th[h % 2]
                thk = "Th%d" % (h % 2)
                T.dma("sp", thk, out=th[:], in_=tab[h], writes=[thk])
                wv, wvk = ws.get(blk_v[h])
                for g4 in range(4):
                    nt = min(4, 15 - 4 * g4)
                    b = ipb[0] % 2
                    ipb[0] += 1
                    bank = pb[b]
                    bkey = "pb%d" % b

                    def mv():
                        last = None
                        for j in range(nt):
                            src, skey, c0 = v_src(4 * g4 + j)
                            for k in range(NKC):
                                last = E["pe"].matmul(bank[:, j * 128:(j + 1) * 128], lhsT=src[:, k, c0:c0 + 128], rhs=wv[:, k, :],
                                                      start=(k == 0), stop=(k == NKC - 1))
                        return last
                    T.op("pe", mv, reads=[wvk, "hTm", "hTh", "hTc"], writes=[bkey])
                    T.op("dve", lambda: E["dve"].tensor_copy(out=Vt[:, 4 * g4:4 * g4 + nt, 0:128],
                                                            in_=bank[:, 0:nt * 128].rearrange("p (s d) -> p s d", d=128)),
                         reads=[bkey], writes=["Vt"])

                nxt = head_items(h)
                yb = ybT[0]
                ybk = "ybT0"

                def qk(qi):
                    jl = QTILES[qi]
                    keys = KEYS[jl]
                    nl = len(keys)
                    ps = pS[qi % 2]

                    def f():
                        last = None
                        for i, t in enumerate(keys):
                            s_ = slot_of(t)
                            E["pe"].matmul(ps[:, i * 128:(i + 1) * 128], lhsT=KT[:, s_ * 128:(s_ + 1) * 128],
                                           rhs=QT[:, qi * 128:(qi + 1) * 128], start=True, stop=False)
                            pi = PAIR_IDX[(jl, t)]
                            last = E["pe"].matmul(ps[:, i * 128:(i + 1) * 128], lhsT=rowsel[:, :],
                                                  rhs=rmask[:, pi * 128:(pi + 1) * 128], start=False, stop=True)
                        for c in range(2):
                            i = nl + c
                            s_ = 13 + c
                            last = E["pe"].matmul(ps[:, i * 128:(i + 1) * 128], lhsT=KT[:, s_ * 128:(s_ + 1) * 128],
                                                  rhs=QT[:, qi * 128:(qi + 1) * 128], start=True, stop=True)
                        return last
                    T.op("pe", f, reads=[KTk, QTk, "rowsel", "rmask"], writes=["pS%d" % (qi % 2)])

                qk(0)
                for qi in range(10):
                    jl = QTILES[qi]
                    keys = KEYS[jl]
                    nl = len(keys)
                    d0 = keys[0] - jl
                    ps = pS[qi % 2]
                    psk = "pS%d" % (qi % 2)
                    s_sb = ssb[qi % 2]
                    ssk = "ssb%d" % (qi % 2)
                    P = Pt[qi % 2]
                    Pk = "Pt%d" % (qi % 2)
                    if qi + 1 < 10:
                        qk(qi + 1)
                    T.op("dve", lambda: E["dve"].tensor_tensor(out=s_sb[:, 0:nl * 128], in0=ps[:, 0:nl * 128],
                                                              in1=th[:, (d0 + 3) * 128:(d0 + 3 + nl) * 128], op=ALU.add),
                         reads=[psk, thk], writes=[ssk])
                    T.op("act", lambda: E["act"].activation(out=P[:, 0:nl * 128], in_=s_sb[:, 0:nl * 128], func=AF.Exp),
                         reads=[ssk], writes=[Pk])
                    T.op("act", lambda: E["act"].activation(out=P[:, nl * 128:(nl + 2) * 128],
                                                            in_=ps[:, nl * 128:(nl + 2) * 128], func=AF.Exp),
                         reads=[psk], writes=[Pk])
                    if nxt:
                        nxt.pop(0)()

                    def pv():
                        last = None
                        sl = [slot_of(t) for t in keys] + [13, 14]
                        for i, s_ in enumerate(sl):
                            last = E["pe"].matmul(pb[3][:, 0:129], lhsT=P[:, i * 128:(i + 1) * 128], rhs=Vt[:, s_, 0:129],
                                                  start=(i == 0), stop=(i == len(sl) - 1))
                        return last
                    T.op("pe", pv, reads=[Pk, "Vt"], writes=["pb3a"])
                    rd = rden[:, qi:qi + 1]
                    T.op("dve", lambda: E["dve"].reciprocal(out=rd, in_=pb[3][:, 128:129]), reads=["pb3a"], writes=["rden%d" % qi])
                    ytm = ybtm[qi % 2]
                    ytk = "ybtm%d" % (qi % 2)
                    T.op("act", lambda: E["act"].activation(out=ytm[:], in_=pb[3][:, 0:128], func=AF.Identity, scale=rd),
                         reads=["pb3a", "rden%d" % qi], writes=[ytk])
                    T.op("pe", lambda: E["pe"].transpose(pb[3][:, 256:384], ytm[:], ident[:]), reads=[ytk, "ident"], writes=["pb3b"])
                    T.op("dve", lambda: E["dve"].tensor_copy(out=yb[:, qi * 128:(qi + 1) * 128], in_=pb[3][:, 256:384]),
                         reads=["pb3b"], writes=[ybk])
                while nxt:
                    nxt.pop(0)()
                T.dma("sp", ybk, out=ycat_d[16 + h], in_=yb[:, 127:1153], reads=[ybk], writes=["ycat_d"])
                if h == 0:
                    dump("ybT0", yb[:], [128, 1280], BF16)
                    dump("KT0", KT[:], [128, 1920], BF16)
                    dump("QT0", QT[:], [128, 1280], BF16)
                    if stop_after == "H0":
                        return finish(nc, es, T, dbg_outs)
            T.barrier()

        with ExitStack() as sg:
            gvtm = sb(sg, "gvtm", [128, 10, 2048], BF16)
            gtmp = [sb(sg, "gtmp%d" % i, [128, 1280], F32) for i in range(2)]
            wsT = sb(sg, "wsT", [128, 16, 128], BF16)
            ones = sb(sg, "ones", [128, 128], BF16)
            Ct = sb(sg, "Ct", [128, 16, 128], F32)
            bsb = sb(sg, "bsb", [128, 2048], F32)
            lng = sb(sg, "lng", [128, 16], F32)
            lnb = sb(sg, "lnb", [128, 16], F32)
            gst = sb(sg, "gst", [128, 80], F32)
            uT = [sb(sg, "uT%d" % i, [128, 1280], BF16) for i in range(2)]
            yaT = [sb(sg, "yaT%d" % i, [128, 1280], BF16) for i in range(2)]
            stmp = [sb(sg, "stmp%d" % i, [128, 512], F32) for i in range(2)]
            gjunk = sb(sg, "gjunk", [128, 2048], BF16)
            T.dma("pool", "wsT", out=wsT[:], in_=wsT_in[:, :, :], writes=["wsT"])
            T.dma("sp", "bsb", out=bsb[:], in_=b_s.partition_broadcast(128), writes=["bsb"])
            with nc.allow_non_contiguous_dma(reason="small vector layouts"):
                T.dma("sp", "lng", out=lng[:], in_=ln_g.rearrange("(b p) -> p b", p=128), writes=["lng"])
                T.dma("sp", "lnb", out=lnb[:], in_=ln_b.rearrange("(b p) -> p b", p=128), writes=["lnb"])
            T.op("dve", lambda: E["dve"].memset(ones[:], 1.0), writes=["ones"])
            T.op("dve", lambda: E["dve"].memset(gst[:], 0.0), writes=["gst"])
            for g4 in range(4):
                bank = pb[g4 % 2]
                bkey = "pb%d" % (g4 % 2)

                def rsum():
                    last = None
                    for j in range(4):
                        g = g4 * 4 + j
                        last = E["pe"].matmul(bank[:, j * 128:(j + 1) * 128], lhsT=ones[:], rhs=wsT[:, g, :], start=True, stop=True)
                    return last
                T.op("pe", rsum, reads=["ones", "wsT"], writes=[bkey])
                for j in range(4):
                    g = g4 * 4 + j
                    T.op("dve", lambda: E["dve"].scalar_tensor_tensor(out=Ct[:, g, :], in0=bank[:, j * 128:(j + 1) * 128],
                                                                      scalar=lnb[:, g:g + 1], in1=bsb[:, g * 128:(g + 1) * 128],
                                                                      op0=ALU.mult, op1=ALU.add),
                         reads=[bkey, "lnb", "bsb"], writes=["Ct"])
            v_blocks = [wblock(w_in, 2048 + 128 * g) for g in range(16)]
            u_blocks = []
            g_ada = []
            for g in range(16):
                u_blocks.append(wblock(w_in, 128 * g))
                g_ada.append([])
            main_chunks = [(hTm, "hTm", 0, 512), (hTm, "hTm", 512, 512), (hTm, "hTm", 1024, 256)]
            def v_lin(g):
                gt = gtmp[g % 2]
                gtk = "gtmp%d" % (g % 2)

                def evg(ci, bank, bkey, n):
                    c0 = main_chunks[ci][2]
                    T.op("act", lambda: E["act"].activation(out=gt[:, c0:c0 + n], in_=bank[:, 0:n], func=AF.Gelu_apprx_tanh),
                         reads=[bkey], writes=[gtk])
                linear(v_blocks[g], main_chunks, evg, banks=(0, 1))

            def v_tr(g):
                gt = gtmp[g % 2]
                gtk = "gtmp%d" % (g % 2)
                for g4 in range(3):
                    nt = min(4, 10 - 4 * g4)
                    bank, bkey = ((pb[3], "pb3"), (pS[1], "pS1"))[g4 % 2]

                    def trg():
                        last = None
                        for j in range(nt):
                            ch = 4 * g4 + j
                            last = E["pe"].transpose(bank[:, j * 128:(j + 1) * 128], gt[:, ch * 128:(ch + 1) * 128], ident[:])
                        return last
                    T.op("pe", trg, reads=[gtk, "ident"], writes=[bkey])
                    T.op("dve", lambda: E["dve"].tensor_copy(out=gvtm[:, 4 * g4:4 * g4 + nt, g * 128:(g + 1) * 128],
                                                            in_=bank[:, 0:nt * 128].rearrange("p (s d) -> p s d", d=128)),
                         reads=[bkey], writes=["gvtm"])

            v_lin(0)
            for g in range(16):
                if g + 1 < 16:
                    v_lin(g + 1)
                v_tr(g)
            for ch in range(10):
                T.op("act", lambda: E["act"].activation(out=gjunk[:], in_=gvtm[:, ch, :], func=AF.Identity,
                                                        accum_out=gst[:, ch:ch + 1]), reads=["gvtm", "gst"], writes=["gjunk", "gs%d" % ch])
                T.op("act", lambda: E["act"].activation(out=gjunk[:], in_=gvtm[:, ch, :], func=AF.Square,
                                                        accum_out=gst[:, 10 + ch:11 + ch]), reads=["gvtm", "gst"], writes=["gjunk", "gq%d" % ch])
            allst = ["gs%d" % ch for ch in range(10)] + ["gq%d" % ch for ch in range(10)]
            mean = gst[:, 20:30]
            var = gst[:, 30:40]
            msq = gst[:, 40:50]
            sdv = gst[:, 50:60]
            rstd = gst[:, 60:70]
            nb = gst[:, 70:80]
            T.op("dve", lambda: E["dve"].tensor_scalar(out=mean, in0=gst[:, 0:10], scalar1=1.0 / 2048, scalar2=None, op0=ALU.mult),
                 reads=allst, writes=["g_mean"])
            T.op("dve", lambda: E["dve"].tensor_tensor(out=msq, in0=mean, in1=mean, op=ALU.mult), reads=["g_mean"], writes=["g_msq"])
            T.op("dve", lambda: E["dve"].scalar_tensor_tensor(out=var, in0=gst[:, 10:20], scalar=1.0 / 2048, in1=msq,
                                                              op0=ALU.mult, op1=ALU.subtract), reads=allst + ["g_msq"], writes=["g_var"])
            T.op("dve", lambda: E["dve"].tensor_scalar(out=var, in0=var, scalar1=EPS, scalar2=None, op0=ALU.add),
                 reads=["g_var"], writes=["g_var"])
            T.op("act", lambda: E["act"].activation(out=sdv, in_=var, func=AF.Sqrt), reads=["g_var"], writes=["g_sd"])
            T.op("dve", lambda: E["dve"].reciprocal(out=rstd, in_=sdv), reads=["g_sd"], writes=["g_rstd"])
            T.op("dve", lambda: E["dve"].scalar_tensor_tensor(out=nb, in0=mean, scalar=-1.0, in1=rstd, op0=ALU.mult, op1=ALU.mult),
                 reads=["g_mean", "g_rstd"], writes=["g_nb"])
            for ch in range(10):
                T.op("act", lambda: E["act"].activation(out=gvtm[:, ch, :], in_=gvtm[:, ch, :], func=AF.Identity,
                                                        bias=gst[:, 70 + ch:71 + ch], scale=gst[:, 60 + ch:61 + ch]),
                     reads=["gvtm", "g_nb", "g_rstd"], writes=["gvtm"])
            dump("gvtm", gvtm[:], [128, 10, 2048], BF16)
            u_chunks = [(hTm, "hTm", 127, 512), (hTm, "hTm", 639, 512), (hTm, "hTm", 1151, 2)]
            for g in range(16):
                u = uT[g % 2]
                uk = "uT%d" % (g % 2)
                ya = yaT[g % 2]
                yak = "yaT%d" % (g % 2)

                def evu(ci, bank, bkey, n):
                    c0 = u_chunks[ci][2]
                    T.op("act", lambda: E["act"].activation(out=u[:, c0:c0 + n], in_=bank[:, 0:n], func=AF.Gelu_apprx_tanh),
                         reads=[bkey], writes=[uk])
                linear(u_blocks[g], u_chunks, evu, banks=(0, 1))
                for g4 in range(3):
                    nt = min(4, 10 - 4 * g4)
                    bank, bkey = ((pb[3], "pb3"), (pS[1], "pS1"))[g4 % 2]
                    st_ = stmp[g4 % 2]
                    stk = "stmp%d" % (g4 % 2)

                    def spm():
                        last = None
                        for j in range(nt):
                            ch = 4 * g4 + j
                            last = E["pe"].matmul(bank[:, j * 128:(j + 1) * 128], lhsT=gvtm[:, ch, g * 128:(g + 1) * 128],
                                                  rhs=wsT[:, g, :], start=True, stop=True)
                        return last
                    T.op("pe", spm, reads=["gvtm", "wsT"], writes=[bkey])
                    for j in range(nt):
                        T.op("dve", lambda: E["dve"].scalar_tensor_tensor(out=st_[:, j * 128:(j + 1) * 128],
                                                                          in0=bank[:, j * 128:(j + 1) * 128],
                                                                          scalar=lng[:, g:g + 1], in1=Ct[:, g, :],
                                                                          op0=ALU.mult, op1=ALU.add),
                             reads=[bkey, "lng", "Ct"], writes=[stk])
                    c0 = 4 * g4 * 128
                    T.op("dve", lambda: E["dve"].tensor_tensor(out=ya[:, c0:c0 + nt * 128], in0=st_[:, 0:nt * 128],
                                                              in1=u[:, c0:c0 + nt * 128], op=ALU.mult),
                         reads=[stk, uk], writes=[yak])
                T.dma("sp", yak, out=ycat_d[g], in_=ya[:, 127:1153], reads=[yak], writes=["ycat_d"])

                if g == 0:
                    dump("yaT0", ya[:], [128, 1280], BF16)
            T.barrier()
            psv2 = pb[2][:, 0:192].rearrange("p (b j) -> p b j", j=2)
            for v in range(2, 5):
                T.op("dve", lambda: E["dve"].tensor_tensor(out=vec[:, v, :], in0=psv2[:, (v - 2) * 32:(v - 1) * 32, 0],
                                                          in1=bada[:, v * 32:(v + 1) * 32], op=ALU.add), writes=["vec%d" % v])
            T.barrier()
    if stop_after == "S2":
        return finish(nc, es, T, dbg_outs)

    w_out = inp("w_out", [D, D])
    with ExitStack() as s3:
        ycT = sb(s3, "ycT", [128, 32, 1026], BF16)
        osb = [sb(s3, "osb%d" % i, [128, 1026], F32) for i in range(2)]
        xp = [sb(s3, "xp%d" % i, [128, 8, 128], F32) for i in range(2)]
        xo = [sb(s3, "xo%d" % i, [128, 8, 128], F32) for i in range(2)]
        xh2 = sb(s3, "xh2", [2, D], F32)
        xho = sb(s3, "xho", [2, D], F32)
        for q4 in range(4):
            T.dma("sp", "ycT", out=ycT[:, q4 * 8:(q4 + 1) * 8, :], in_=ycat_d[q4 * 8:(q4 + 1) * 8].rearrange("m p t -> p m t"),
                  reads=["ycat_d"], writes=["ycT"])
        T.dma("sp", "xh2", out=xh2[0:1, :], in_=xa[127:128, :], writes=["xh2"])
        T.dma("sp", "xh2", out=xh2[1:2, :], in_=xa[1152:1153, :], writes=["xh2"])
        o_blocks = [wblock(w_out, 128 * db) for db in range(32)]
        x_own = xa[128:1152, :].rearrange("(j p) d -> p j d", p=128)
        xn_own = xnew_d[1:1025, :].rearrange("(j p) d -> p j d", p=128)
        def s3_mm(db):
            wb, wkey = ws.get(o_blocks[db])
            ps = pS[db % 2]
            psk = "pS%d" % (db % 2)
            hb = pb[2 + db % 2]
            hbk = "pb%d" % (2 + db % 2)
            x_p = xp[db % 2]
            xpk = "xp%d" % (db % 2)
            T.dma("sp", xpk, out=x_p[:], in_=x_own[:, :, db * 128:(db + 1) * 128], writes=[xpk])

            def mm():
                last = None
                for hf in range(2):
                    for k in range(NKC):
                        last = E["pe"].matmul(ps[:, hf * 512:(hf + 1) * 512], lhsT=wb[:, k, :], rhs=ycT[:, k, 1 + hf * 512:1 + (hf + 1) * 512],
                                              start=(k == 0), stop=(k == NKC - 1))
                for k in range(NKC):
                    last = E["pe"].matmul(hb[:, 2 * db:2 * db + 2], lhsT=wb[:, k, :], rhs=ycT[:, k, 0:1026:1025],
                                          start=(k == 0), stop=(k == NKC - 1))
                return last
            T.op("pe", mm, reads=[wkey, "ycT"], writes=[psk, hbk])

        def s3_rest(db):
            ps = pS[db % 2]
            psk = "pS%d" % (db % 2)
            hb = pb[2 + db % 2]
            hbk = "pb%d" % (2 + db % 2)
            o = osb[db % 2]
            ok = "osb%d" % (db % 2)
            x_p = xp[db % 2]
            xpk = "xp%d" % (db % 2)
            x_o = xo[db % 2]
            xok = "xo%d" % (db % 2)
            gt1 = vec[:, V_GT1, db:db + 1]
            T.op("act", lambda: E["act"].activation(out=o[:, 1:1025], in_=ps[:, 0:1024], func=AF.Identity, scale=gt1),
                 reads=[psk], writes=[ok])
            T.op("act", lambda: E["act"].activation(out=o[:, 0:1026:1025], in_=hb[:, 2 * db:2 * db + 2], func=AF.Identity, scale=gt1),
                 reads=[hbk], writes=[ok])
            for g4 in range(2):
                bank = pb[g4]
                bkey = "pb%d" % g4

                def tr():
                    last = None
                    for j in range(4):
                        tj = 4 * g4 + j
                        last = E["pe"].transpose(bank[:, j * 128:(j + 1) * 128], o[:, 1 + tj * 128:1 + (tj + 1) * 128], ident[:])
                    return last
                T.op("pe", tr, reads=[ok, "ident"], writes=[bkey])
                T.op("dve", lambda: E["dve"].tensor_tensor(out=x_o[:, 4 * g4:4 * g4 + 4, :],
                                                          in0=bank[:, 0:512].rearrange("p (s d) -> p s d", d=128),
                                                          in1=x_p[:, 4 * g4:4 * g4 + 4, :], op=ALU.add),
                     reads=[bkey, xpk], writes=[xok])
            T.op("pe", lambda: E["pe"].transpose(hb[0:2, 256:384], o[:, 0:1026:1025], ident[:]), reads=[ok, "ident"], writes=[hbk])
            T.op("dve", lambda: E["dve"].tensor_tensor(out=xho[0:2, db * 128:(db + 1) * 128], in0=hb[0:2, 256:384],
                                                      in1=xh2[0:2, db * 128:(db + 1) * 128], op=ALU.add),
                 reads=[hbk, "xh2"], writes=["xho"])
            T.dma("sp", xok, out=xn_own[:, :, db * 128:(db + 1) * 128], in_=x_o[:], reads=[xok], writes=["xnew_d"])

        s3_mm(0)
        for db in range(32):
            if db + 1 < 32:
                s3_mm(db + 1)
            s3_rest(db)
        T.dma("sp", "xho", out=xnew_d[0:1, :], in_=xho[0:1, :], reads=["xho"], writes=["xnew_d"])
        T.dma("sp", "xho", out=xnew_d[1025:1026, :], in_=xho[1:2, :], reads=["xho"], writes=["xnew_d"])
        T.barrier()
    if "xnew" in dbg.get("dump", ()):
        dbg_outs.append("xnew_d")
    if stop_after == "S3":
        return finish(nc, es, T, dbg_outs)

    w_up = inp("w_up", [D, 2 * DFF])
    conv_w = inp("conv_w", [3, DFF])
    conv_b = inp("conv_b", [DFF])
    w_down = inp("w_down", [DFF, D])
    gf = inp("g_final", [D])
    with ExitStack() as s45:
        h2T = sb(s45, "h2T", [128, NKC, 1026], BF16)
        cw = sb(s45, "cw", [128, 3, NFB], F32)
        cb_ = sb(s45, "cb", [128, NFB], F32)
        with nc.allow_non_contiguous_dma(reason="small vector layouts"):
            T.dma("sp", "cw", out=cw[:], in_=conv_w.rearrange("t (b p) -> p t b", p=128), writes=["cw"])
            T.dma("sp", "cb", out=cb_[:], in_=conv_b.rearrange("(b p) -> p b", p=128), writes=["cb"])
        with ExitStack() as s4:
            T.op("dve", lambda: E["dve"].scalar_tensor_tensor(out=vec[:, V_A2, :], in0=vec[:, V_SC2, :], scalar=1.0,
                                                              in1=gfb[:], op0=ALU.add, op1=ALU.mult), writes=["vecA2"])
            T.barrier()
            tiles = []
            for j in range(8):
                f = (lambda blk, j=j: h2T[:, blk, 1 + j * 128:1 + (j + 1) * 128])
                tiles.append((xnew_d[1 + j * 128:1 + (j + 1) * 128, :], 128, f, "h2T", V_A2, V_SH2))
            norm_tiles(s4, tiles, "n2", nbuf=4)
            xt2 = sb(s4, "xt2", [2, D], F32)
            junk2 = sb(s4, "junk2", [2, D], BF16)
            st2 = sb(s4, "st2", [2, 4], F32)
            T.op("dve", lambda: E["dve"].memset(st2[:], 0.0), writes=["st2"])
            T.dma("sp", "xt2", out=xt2[0:1, :], in_=xnew_d[0:1, :], writes=["xt2"])
            T.dma("sp", "xt2", out=xt2[1:2, :], in_=xnew_d[1025:1026, :], writes=["xt2"])
            T.op("act", lambda: E["act"].activation(out=junk2[:], in_=xt2[:], func=AF.Square, accum_out=st2[:, 0:1]),
                 reads=["xt2", "st2"], writes=["junk2", "st2a"])
            T.op("dve", lambda: E["dve"].tensor_scalar(out=st2[:, 1:2], in0=st2[:, 0:1], scalar1=1.0 / D, scalar2=EPS,
                                                      op0=ALU.mult, op1=ALU.add), reads=["st2a"], writes=["st2b"])
            T.op("act", lambda: E["act"].activation(out=st2[:, 2:3], in_=st2[:, 1:2], func=AF.Sqrt), reads=["st2b"], writes=["st2c"])
            T.op("dve", lambda: E["dve"].reciprocal(out=st2[:, 3:4], in_=st2[:, 2:3]), reads=["st2c"], writes=["st2d"])
            T.op("dve", lambda: E["dve"].tensor_scalar(out=xt2[:], in0=xt2[:], scalar1=st2[:, 3:4], scalar2=None, op0=ALU.mult),
                 reads=["st2d", "xt2"], writes=["xt2"])
            for gq in range(8):
                bank = pb[gq % 4]
                bkey = "pb%d" % (gq % 4)

                def tr():
                    last = None
                    for j in range(4):
                        blk = gq * 4 + j
                        last = E["pe"].transpose(bank[:, j * 2:(j + 1) * 2], xt2[0:2, blk * 128:(blk + 1) * 128], ident[0:2, 0:2])
                    return last
                T.op("pe", tr, reads=["xt2", "ident"], writes=[bkey])
                for j in range(4):
                    blk = gq * 4 + j
                    T.op("dve", lambda: E["dve"].tensor_scalar(out=h2T[:, blk, 0:1026:1025], in0=bank[:, j * 2:(j + 1) * 2],
                                                              scalar1=vec[:, V_A2, blk:blk + 1], scalar2=vec[:, V_SH2, blk:blk + 1],
                                                              op0=ALU.mult, op1=ALU.add), reads=[bkey], writes=["h2T"])
            T.op("dve", lambda: E["dve"].tensor_scalar(out=h2T[:, :, 0], in0=h2T[:, :, 0], scalar1=flags[:, 0:1], scalar2=None, op0=ALU.mult),
                 reads=["h2T", "flags"], writes=["h2T"])
            T.op("dve", lambda: E["dve"].tensor_scalar(out=h2T[:, :, 1025], in0=h2T[:, :, 1025], scalar1=flags[:, 1:2], scalar2=None, op0=ALU.mult),
                 reads=["h2T", "flags"], writes=["h2T"])
            T.barrier()
        dump("h2T", h2T[:], [128, NKC, 1026], BF16)
        if stop_after == "S4":
            return finish(nc, es, T, dbg_outs)

        xn_own = xnew_d[1:1025, :].rearrange("(j p) d -> p j d", p=128)
        y_own = y_d.rearrange("(j p) d -> p j d", p=128)
        MAXF = max(n_ for _, n_ in PARTS)
        zT = sb(s45, "zT", [128, MAXF, 1024], BF16)
        asb = [sb(s45, "asb%d" % i, [128, 1026], F32) for i in range(2)]
        ct = [sb(s45, "ct%d" % i, [128, 1024], F32) for i in range(2)]
        dosb = [sb(s45, "dosb%d" % i, [128, 1024], F32) for i in range(2)]
        pin = [sb(s45, "pin%d" % i, [128, 1024], F32) for i in range(2)]
        dxp = [sb(s45, "dxp%d" % i, [128, 8, 128], F32) for i in range(2)]
        dyo = [sb(s45, "dyo%d" % i, [128, 8, 128], F32) for i in range(2)]
        wdv = w_down.rearrange("(fb p) d -> p fb d", p=128)
        up_blocks = {}
        d_blocks = {}
        ffn_ada = []
        for pp, (fb0, nfb) in enumerate(PARTS):
            up_blocks[pp] = []
            for f in range(nfb):
                up_blocks[pp].append((wblock(w_up, (fb0 + f) * 128), wblock(w_up, DFF + (fb0 + f) * 128)))
                if pp == 0:
                    for _ in range(2 if f < 10 else 1):
                        ffn_ada.append((f, len(ffn_ada), wblock(w_ada, (160 + len(ffn_ada)) * 128)))
            d_blocks[pp] = [ws.add(wdv[:, fb0:fb0 + nfb, db * 128:(db + 1) * 128], nfb) for db in range(32)]
        hctr = 0
        for pp, (fb0, nfb) in enumerate(PARTS):
            last_part = (pp == NPARTS - 1)
            for f in range(nfb):
                fb = fb0 + f
                wa, wak = ws.get(up_blocks[pp][f][0], ahead=2)
                wg, wgk = ws.get(up_blocks[pp][f][1], ahead=2)
                a_s = asb[f % 2]
                ak = "asb%d" % (f % 2)
                c_t = ct[f % 2]
                ck = "ct%d" % (f % 2)
                pg = pS[f % 2]
                pgk = "pS%d" % (f % 2)
                hcol = 2 * (hctr % 128)
                hctr += 1

                def mh():
                    last = None
                    for k in range(NKC):
                        last = E["pe"].matmul(pb[2][:, hcol:hcol + 2], lhsT=wa[:, k, :], rhs=h2T[:, k, 0:1026:1025],
                                              start=(k == 0), stop=(k == NKC - 1))
                    return last
                T.op("pe", mh, reads=[wak, "h2T"], writes=["pb2"])
                T.op("act", lambda: E["act"].copy(out=a_s[:, 0:1026:1025], in_=pb[2][:, hcol:hcol + 2]),
                     reads=["pb2"], writes=[ak])
                for hf in range(2):
                    pa = pb[hf]
                    pak = "pb%d" % hf

                    def ma():
                        last = None
                        for k in range(NKC):
                            last = E["pe"].matmul(pa[:, 0:512], lhsT=wa[:, k, :], rhs=h2T[:, k, 1 + hf * 512:1 + (hf + 1) * 512],
                                                  start=(k == 0), stop=(k == NKC - 1))
                        return last
                    T.op("pe", ma, reads=[wak, "h2T"], writes=[pak])
                    T.op("act", lambda: E["act"].copy(out=a_s[:, 1 + hf * 512:1 + (hf + 1) * 512], in_=pa[:, 0:512]),
                         reads=[pak], writes=[ak])

                    def mg():
                        last = None
                        for k in range(NKC):
                            last = E["pe"].matmul(pg[:, hf * 512:(hf + 1) * 512], lhsT=wg[:, k, :],
                                                  rhs=h2T[:, k, 1 + hf * 512:1 + (hf + 1) * 512],
                                                  start=(k == 0), stop=(k == NKC - 1))
                        return last
                    T.op("pe", mg, reads=[wgk, "h2T"], writes=[pgk])
                T.op("dve", lambda: E["dve"].tensor_scalar(out=c_t[:], in0=a_s[:, 1:1025], scalar1=cw[:, 1, fb:fb + 1],
                                                          scalar2=cb_[:, fb:fb + 1], op0=ALU.mult, op1=ALU.add),
                     reads=[ak, "cw", "cb"], writes=[ck])
                T.op("dve", lambda: E["dve"].scalar_tensor_tensor(out=c_t[:], in0=a_s[:, 0:1024], scalar=cw[:, 0, fb:fb + 1],
                                                                  in1=c_t[:], op0=ALU.mult, op1=ALU.add),
                     reads=[ak, ck], writes=[ck])
                T.op("dve", lambda: E["dve"].scalar_tensor_tensor(out=c_t[:], in0=a_s[:, 2:1026], scalar=cw[:, 2, fb:fb + 1],
                                                                  in1=c_t[:], op0=ALU.mult, op1=ALU.add),
                     reads=[ak, ck], writes=[ck])
                T.op("act", lambda: E["act"].activation(out=c_t[:], in_=c_t[:], func=AF.Silu), reads=[ck], writes=[ck])
                T.op("dve", lambda: E["dve"].tensor_tensor(out=zT[:, f, :], in0=c_t[:], in1=pg[:, 0:1024], op=ALU.mult),
                     reads=[ck, pgk], writes=["zT%d" % f])
                if pp == 0:
                    for (f_, j_, bi_) in ffn_ada:
                        if f_ == f:
                            ada_group([bi_], 160 + j_, pb[3][:, 2 * j_:2 * j_ + 2], "pb3", evac=False)
            if pp == 0:
                psv3 = pb[3][:, 0:64].rearrange("p (b j) -> p b j", j=2)
                T.op("dve", lambda: E["dve"].tensor_tensor(out=vec[:, V_GT2, :], in0=psv3[:, :, 0], in1=bada[:, 160:192], op=ALU.add),
                     reads=["pb3", "bada"], writes=["vec5"])
                dump("zT0", zT[:, 0, :], [128, 1024], BF16)
            if dbg.get("ffn_bar", True):
                T.barrier()
            zkeys = ["zT%d" % f for f in range(nfb)]
            def d_mm(db):
                wd, wdk = ws.get(d_blocks[pp][db])
                ps = pS[db % 2]
                psk = "pS%d" % (db % 2)
                if pp > 0:
                    T.dma("sp", "pin%d" % (db % 2), out=pin[db % 2][:], in_=part_d[db], reads=["part_d%d" % db], writes=["pin%d" % (db % 2)])
                if last_part:
                    T.dma("sp", "dxp%d" % (db % 2), out=dxp[db % 2][:], in_=xn_own[:, :, db * 128:(db + 1) * 128], writes=["dxp%d" % (db % 2)])

                def md():
                    last = None
                    for hf in range(2):
                        for f in range(nfb):
                            last = E["pe"].matmul(ps[:, hf * 512:(hf + 1) * 512], lhsT=wd[:, f, :], rhs=zT[:, f, hf * 512:(hf + 1) * 512],
                                                  start=(f == 0), stop=(f == nfb - 1))
                    return last
                T.op("pe", md, reads=[wdk] + zkeys, writes=[psk])

            def d_rest(db):
                ps = pS[db % 2]
                psk = "pS%d" % (db % 2)
                o = dosb[db % 2]
                ok = "dosb%d" % (db % 2)
                p_i = pin[db % 2]
                pik = "pin%d" % (db % 2)
                x_p = dxp[db % 2]
                xpk = "dxp%d" % (db % 2)
                y_o = dyo[db % 2]
                yok = "dyo%d" % (db % 2)
                if pp == 0:
                    T.op("act", lambda: E["act"].copy(out=o[:], in_=ps[:, 0:1024]), reads=[psk], writes=[ok])
                else:
                    T.op("dve", lambda: E["dve"].tensor_tensor(out=o[:], in0=ps[:, 0:1024], in1=p_i[:], op=ALU.add),
                         reads=[psk, pik], writes=[ok])
                if not last_part:
                    T.dma("sp", ok, out=part_d[db], in_=o[:], reads=[ok], writes=["part_d%d" % db])
                else:
                    T.op("act", lambda: E["act"].activation(out=o[:], in_=o[:], func=AF.Identity, scale=vec[:, V_GT2, db:db + 1]),
                         reads=[ok, "vec5"], writes=[ok])
                    for g4 in range(2):
                        bank = pb[g4]
                        bkey = "pb%d" % g4

                        def tr():
                            last = None
                            for j in range(4):
                                tj = 4 * g4 + j
                                last = E["pe"].transpose(bank[:, j * 128:(j + 1) * 128], o[:, tj * 128:(tj + 1) * 128], ident[:])
                            return last
                        T.op("pe", tr, reads=[ok, "ident"], writes=[bkey])
                        T.op("dve", lambda: E["dve"].tensor_tensor(out=y_o[:, 4 * g4:4 * g4 + 4, :],
                                                                  in0=bank[:, 0:512].rearrange("p (s d) -> p s d", d=128),
                                                                  in1=x_p[:, 4 * g4:4 * g4 + 4, :], op=ALU.add),
                             reads=[bkey, xpk], writes=[yok])
                    T.dma("sp", yok, out=y_own[:, :, db * 128:(db + 1) * 128], in_=y_o[:], reads=[yok], writes=["y_d"])

            d_mm(0)
            for db in range(32):
                if db + 1 < 32:
                    d_mm(db + 1)
                d_rest(db)
            if dbg.get("ffn_bar", True) and not last_part:
                T.barrier()
        T.barrier()
    if stop_after == "S5":
        if "y" in dbg.get("dump", ()):
            dbg_outs.append("y_d")
        return finish(nc, es, T, dbg_outs)

    with ExitStack() as s6:
        gfull = sb(s6, "gfull", [128, D], F32)
        yt = [sb(s6, "yt%d" % i, [128, D], F32) for i in range(4)]
        junk = sb(s6, "fjunk", [128, D], BF16)
        st = sb(s6, "fst", [128, 32], F32)
        T.dma("sp", "gfull", out=gfull[:], in_=gf.partition_broadcast(128), writes=["gfull"])
        T.op("dve", lambda: E["dve"].memset(st[:], 0.0), writes=["fst"])
        for j in range(8):
            y_t = yt[j % 4]
            yk = "yt%d" % (j % 4)
            T.dma("sp", yk, out=y_t[:], in_=y_d[j * 128:(j + 1) * 128, :], writes=[yk])
            ss = st[:, 4 * j:4 * j + 1]
            ms = st[:, 4 * j + 1:4 * j + 2]
            sd = st[:, 4 * j + 2:4 * j + 3]
            rs = st[:, 4 * j + 3:4 * j + 4]
            sk = "fs%d" % j
            T.op("act", lambda: E["act"].activation(out=junk[:], in_=y_t[:], func=AF.Square, accum_out=ss),
                 reads=[yk, "fst"], writes=["fjunk", sk])
            T.op("dve", lambda: E["dve"].tensor_scalar(out=ms, in0=ss, scalar1=1.0 / D, scalar2=EPS, op0=ALU.mult, op1=ALU.add),
                 reads=[sk], writes=[sk + "m"])
            T.op("act", lambda: E["act"].activation(out=sd, in_=ms, func=AF.Sqrt), reads=[sk + "m"], writes=[sk + "s"])
            T.op("dve", lambda: E["dve"].reciprocal(out=rs, in_=sd), reads=[sk + "s"], writes=[sk + "r"])
            T.op("dve", lambda: E["dve"].scalar_tensor_tensor(out=y_t[:], in0=y_t[:], scalar=rs, in1=gfull[:], op0=ALU.mult, op1=ALU.mult),
                 reads=[sk + "r", yk, "gfull"], writes=[yk])
            T.dma("sp", yk, out=out[j * 128:(j + 1) * 128, :], in_=y_t[:], reads=[yk], writes=["out"])
        T.barrier()
    return finish(nc, es, T, dbg_outs)


def finish(nc, es, T, dbg_outs):
    T.barrier()
    nc._dbg_outs = dbg_outs
    return nc


def _tables(rpb):
    k = np.arange(128)
    kr, kc = k // 64, k % 64
    q = np.arange(128)
    qr, qc = q // 64, q % 64
    c0 = np.clip(qc - 8, 0, 48)
    colok = (kc[:, None] >= c0[None, :]) & (kc[:, None] < c0[None, :] + 16)
    dc = np.clip(kc[:, None] - qc[None, :], -15, 15) + 15
    tabs = np.empty((16, 128, 7, 128), np.float32)
    for dti in range(7):
        dr = 2 * (dti - 3) + kr[:, None] - qr[None, :] + 7
        g = rpb[:, dr, dc]
        tabs[:, :, dti, :] = np.where(colok[None], g, np.float32(NEG))
    return tabs.reshape(16, 128, 7 * 128)


def _rmask(core):
    m = np.zeros((2, NPAIR, 2, 64), np.float32)
    for pi, (jl, t) in enumerate(PAIRS):
        for kr in range(2):
            for qr in range(2):
                qrow = 2 * (8 * core + jl) + qr
                krow = 2 * (8 * core + t) + kr
                r0 = min(max(qrow - 4, 0), 120) if 0 <= qrow <= 127 else qrow - 4
                ok = (r0 <= krow <= r0 + 7) and (0 <= krow <= 127)
                m[kr, pi, qr, :] = 0.0 if ok else NEG
    return m.reshape(2, NPAIR * 128)


def make_in_maps(inputs, cores):
    f = lambda a: np.ascontiguousarray(np.asarray(a, dtype=np.float32))
    x = f(inputs["x"])[0]
    shared = {
        "ctx": f(inputs["ctx"])[0],
        "c2": np.ascontiguousarray(np.stack([f(inputs["c"])[0], f(inputs["c_ctx"])])),
        "w_ada": f(inputs["w_ada"])[0], "b_ada": f(inputs["b_ada"])[0], "g_norm1": f(inputs["g_norm1"])[0],
        "w_in": f(inputs["w_in"])[0], "a_ln_g": f(inputs["a_ln_g"])[0], "a_ln_b": f(inputs["a_ln_b"])[0],
        "wsT": np.ascontiguousarray(f(inputs["a_w_s"])[0].transpose(2, 0, 1)),
        "a_b_s": f(inputs["a_b_s"])[0].reshape(-1),
        "tab": _tables(f(inputs["na_rpb"])[0]),
        "rowsel": np.ascontiguousarray(np.stack([(np.arange(128) < 64), (np.arange(128) >= 64)]).astype(np.float32)),
        "ident": np.eye(128, dtype=np.float32),
        "w_out": f(inputs["w_out"])[0], "g_norm2": f(inputs["g_norm2"])[0], "w_up": f(inputs["w_up"])[0],
        "conv_w": f(inputs["conv_w"])[0], "conv_b": f(inputs["conv_b"])[0], "w_down": f(inputs["w_down"])[0],
        "g_final": f(inputs["g_final"]),
    }
    maps = []
    order = list(range(-1, 9)) + [-3, -2, 9]
    for core in cores:
        xa = np.zeros((1664, D), np.float32)
        for s, t in enumerate(order):
            g = 8 * core + t
            if 0 <= g < 64:
                xa[s * 128:(s + 1) * 128] = x[g * 128:(g + 1) * 128]
        flags = np.zeros((128, 2), np.float32)
        flags[:, 0] = 1.0 if core > 0 else 0.0
        flags[:, 1] = 1.0 if core < NCORES - 1 else 0.0
        m = dict(shared)
        m["xa"] = xa
        m["rmask"] = _rmask(core)
        m["flags"] = flags
        maps.append(m)
    return maps


def kernel(**inputs):
    nc = build_nc()
    maps = make_in_maps(inputs, list(range(NCORES)))
    maps = [{k: m[k] for k in nc._in_names} for m in maps]
    res = run_bass_kernel_spmd(nc, maps, core_ids=list(range(NCORES)))
    outs = [np.asarray(r["out"], dtype=np.float32) for r in res.results]
    return np.concatenate(outs, axis=0).reshape(1, NCORES * 1024, D)
```
